# Optimizing a Trainium2 kernel written in Bass

```python
import math
import jax, jax.numpy as jnp
from jax import lax
import numpy as np

D_MODEL = 2048
BATCH = 4
SEQ = 2048
DEPTH = 1

N_META = 16
EPS = 1e-6
DA_HEADS = 4
DA_QK_DIM = 128
DA_V_DIM = 2 * DA_QK_DIM
DA_WIDTH = DA_HEADS * DA_V_DIM
DA_QK_COLS = DA_HEADS * 2 * DA_QK_DIM
Q_BLOCK = 128
ML_HEADS = 4
ML_QK_DIM = 128
ML_V_DIM = 256
ML_WIDTH = ML_HEADS * ML_V_DIM
ML_QK_COLS = ML_HEADS * ML_QK_DIM
ML_CONV = 4
ML_CHUNK = 64
NEG_BIG = -1e30
D_MIX = DA_WIDTH + ML_WIDTH
D_IN_PROJ = 2 * DA_QK_COLS + DA_WIDTH + 2 * ML_QK_COLS + 2 * ML_WIDTH + 2 * ML_HEADS
PEER_HEADS = 8
PEER_KEYS = 128
PEER_N = PEER_KEYS * PEER_KEYS
PEER_DKEY = 256
PEER_TOPK = 16
PEER_BLOCK = 64

kernel_name = "hymba_diffattn_mlstm_peer"


def rms_norm(x, g):
    xf = x.astype(jnp.float32)
    y = xf * lax.rsqrt(jnp.mean(xf * xf, axis=-1, keepdims=True) + EPS)
    return (y * g.astype(jnp.float32)).astype(x.dtype)


def alibi_slopes(n):
    start = 2.0 ** (-8.0 / n)
    return jnp.asarray([start ** (h + 1) for h in range(n)], dtype=jnp.float32)


def diff_attention(q, k, v, lam, lambda_init, qn_g, kn_g, subln_g):
    B, T, H, _, dk = q.shape
    dv = v.shape[-1]
    q = rms_norm(q, qn_g) * (dk ** -0.5)
    k = rms_norm(k, kn_g)
    n_blk = -(-T // Q_BLOCK)
    Tp = n_blk * Q_BLOCK
    qp = jnp.pad(q, ((0, 0), (0, Tp - T), (0, 0), (0, 0), (0, 0)))
    qb = jnp.moveaxis(qp.reshape(B, n_blk, Q_BLOCK, H, 2, dk), 1, 0)
    slopes = alibi_slopes(H)[None, :, None, None, None]
    kpos = jnp.arange(T)

    def block(args):
        qblk, start = args
        s = jnp.einsum('bqhmd,bkhmd->bhmqk', qblk, k,
                       preferred_element_type=jnp.float32)
        qpos = start + jnp.arange(Q_BLOCK)
        dist = (qpos[:, None] - kpos[None, :]).astype(jnp.float32)
        s = s - slopes * dist
        s = jnp.where(dist >= 0, s, -jnp.inf)
        p = jax.nn.softmax(s, axis=-1)
        a = p[:, :, 0] - lam * p[:, :, 1]
        return jnp.einsum('bhqk,bkhd->bqhd', a.astype(v.dtype), v)

    starts = jnp.arange(n_blk) * Q_BLOCK
    o = lax.map(block, (qb, starts))
    o = jnp.moveaxis(o, 0, 1).reshape(B, Tp, H, dv)[:, :T]
    o = rms_norm(o, subln_g) * (1.0 - lambda_init)
    return o.reshape(B, T, H * dv)


def causal_dwconv(x, w, b):
    C = x.shape[-1]
    y = lax.conv_general_dilated(
        x, w[:, None, :].astype(x.dtype), window_strides=(1,),
        padding=((ML_CONV - 1, 0),), dimension_numbers=('NWC', 'WIO', 'NWC'),
        feature_group_count=C)
    return y + b.astype(x.dtype)


def mlstm_chunkwise(q, k, v, li, lf):
    B, T, H, dk = q.shape
    dv = v.shape[-1]
    pad = ML_CHUNK - N_META

    def padf(a, val):
        return jnp.pad(a, ((0, 0), (pad, 0)) + ((0, 0),) * (a.ndim - 2), constant_values=val)

    q = padf(q, 0.0)
    k = padf(k, 0.0) * (dk ** -0.5)
    v = padf(v, 0.0)
    li = padf(li, NEG_BIG)
    lf = padf(lf, 0.0)
    Tp = T + pad
    nc = Tp // ML_CHUNK
    L = ML_CHUNK

    def chunks(a):
        a = a.reshape((B, nc, L) + a.shape[2:])
        return jnp.moveaxis(a, (1, 2), (0, 3))

    causal = jnp.tril(jnp.ones((L, L), dtype=bool))

    def step(carry, inp):
        C, n, m = carry
        qc, kc, vc, lic, lfc = inp
        b = jnp.cumsum(lfc, axis=-1)
        D = b[..., :, None] - b[..., None, :] + lic[..., None, :]
        D = jnp.where(causal, D, -jnp.inf)
        inter = b + m[..., None]
        m_t = jnp.maximum(inter, jnp.max(D, axis=-1))
        w_inter = jnp.exp(inter - m_t)
        S = jnp.einsum('bhtd,bhsd->bhts', qc, kc) * jnp.exp(D - m_t[..., None])
        num = (w_inter[..., None] * jnp.einsum('bhtd,bhde->bhte', qc, C)
               + jnp.einsum('bhts,bhse->bhte', S, vc))
        den = w_inter * jnp.einsum('bhtd,bhd->bht', qc, n) + jnp.sum(S, axis=-1)
        h = num / jnp.maximum(jnp.abs(den), jnp.exp(-m_t))[..., None]
        bL = b[..., -1]
        g = bL[..., None] - b + lic
        m_new = jnp.maximum(bL + m, jnp.max(g, axis=-1))
        decay = jnp.exp(bL + m - m_new)
        wg = jnp.exp(g - m_new[..., None])
        C = decay[..., None, None] * C + jnp.einsum('bhs,bhsd,bhse->bhde', wg, kc, vc)
        n = decay[..., None] * n + jnp.einsum('bhs,bhsd->bhd', wg, kc)
        return (C, n, m_new), h

    init = (jnp.zeros((B, H, dk, dv), jnp.float32),
            jnp.zeros((B, H, dk), jnp.float32),
            jnp.zeros((B, H), jnp.float32))
    _, hs = lax.scan(step, init, (chunks(q), chunks(k), chunks(v), chunks(li), chunks(lf)))
    hs = jnp.moveaxis(hs, (0, 3), (1, 2)).reshape(B, Tp, H, dv)
    return hs[:, pad:]


def token_mixer(xn, w_in, ml_conv_w, ml_conv_b, ml_igate_b, ml_fgate_b,
                da_lq1, da_lk1, da_lq2, da_lk2, da_qnorm_g, da_knorm_g, da_subln_g,
                ml_outnorm_g, w_out, lambda_init):
    B, T, _ = xn.shape
    proj = jnp.einsum('btd,de->bte', xn, w_in)
    sizes = [DA_QK_COLS, DA_QK_COLS, DA_WIDTH, 2 * ML_QK_COLS, ML_WIDTH, ML_WIDTH]
    da_q, da_k, da_v, ml_qk, ml_v, ml_o, ml_gates = jnp.split(
        proj, np.cumsum(sizes).tolist(), axis=-1)
    lam = (jnp.exp(jnp.sum(da_lq1.astype(jnp.float32) * da_lk1.astype(jnp.float32)))
           - jnp.exp(jnp.sum(da_lq2.astype(jnp.float32) * da_lk2.astype(jnp.float32)))
           + lambda_init)
    a_out = diff_attention(
        da_q.reshape(B, T, DA_HEADS, 2, DA_QK_DIM),
        da_k.reshape(B, T, DA_HEADS, 2, DA_QK_DIM),
        da_v.reshape(B, T, DA_HEADS, DA_V_DIM),
        lam, lambda_init, da_qnorm_g, da_knorm_g, da_subln_g)
    qk = jax.nn.silu(causal_dwconv(ml_qk, ml_conv_w, ml_conv_b))
    mq, mk = jnp.split(qk, 2, axis=-1)
    gates = ml_gates.astype(jnp.float32)
    li = gates[..., :ML_HEADS] + ml_igate_b.astype(jnp.float32)
    lf = jax.nn.log_sigmoid(gates[..., ML_HEADS:] + ml_fgate_b.astype(jnp.float32))
    hm = mlstm_chunkwise(
        mq.reshape(B, T, ML_HEADS, ML_QK_DIM).astype(jnp.float32),
        mk.reshape(B, T, ML_HEADS, ML_QK_DIM).astype(jnp.float32),
        ml_v.reshape(B, T, ML_HEADS, ML_V_DIM).astype(jnp.float32), li, lf)
    hm = rms_norm(hm, ml_outnorm_g).astype(xn.dtype).reshape(B, T, ML_WIDTH)
    m_out = jax.nn.sigmoid(ml_o) * hm
    return jnp.einsum('bte,ed->btd', jnp.concatenate([a_out, m_out], axis=-1), w_out)


def peer_ffn(xn, wq, subkeys, u, v):
    B, T, D = xn.shape
    half = PEER_DKEY // 2
    q = jnp.einsum('btd,de->bte', xn, wq).reshape(B, T, PEER_HEADS, 2, half)
    s = jnp.einsum('bthpc,hpkc->bthpk', q, subkeys,
                   preferred_element_type=jnp.float32)
    sv, si = lax.top_k(s, PEER_TOPK)
    cand = (sv[..., 0, :, None] + sv[..., 1, None, :]).reshape(B, T, PEER_HEADS, PEER_TOPK * PEER_TOPK)
    cidx = (si[..., 0, :, None] * PEER_KEYS + si[..., 1, None, :]).reshape(B, T, PEER_HEADS, PEER_TOPK * PEER_TOPK)
    top_v, top_pos = lax.top_k(cand, PEER_TOPK)
    idx = jnp.take_along_axis(cidx, top_pos, axis=-1)
    g = jax.nn.softmax(top_v, axis=-1)
    E = PEER_HEADS * PEER_TOPK
    n_tok = B * T
    nb = -(-n_tok // PEER_BLOCK)
    padn = nb * PEER_BLOCK - n_tok
    xf = jnp.pad(xn.reshape(n_tok, D), ((0, padn), (0, 0))).reshape(nb, PEER_BLOCK, D)
    idf = jnp.pad(idx.reshape(n_tok, E), ((0, padn), (0, 0))).reshape(nb, PEER_BLOCK, E)
    gf = jnp.pad(g.reshape(n_tok, E), ((0, padn), (0, 0))).reshape(nb, PEER_BLOCK, E)

    def block(args):
        xb, ib, gb = args
        act = jax.nn.gelu(jnp.einsum('nd,ned->ne', xb, u[ib]), approximate=False)
        return jnp.einsum('ne,ned->nd', (act.astype(jnp.float32) * gb).astype(xb.dtype), v[ib])

    y = lax.map(block, (xf, idf, gf)).reshape(nb * PEER_BLOCK, D)[:n_tok]
    return y.reshape(B, T, D)


def setup_inputs(seed: int = 0) -> dict:
    key = jax.random.key(seed)
    ks = jax.random.split(key, 24)
    f32 = jnp.float32
    nrm = lambda k, shape, scale: jax.random.normal(k, shape, f32) * scale
    return {
        "x": nrm(ks[0], (BATCH, SEQ, D_MODEL), 1.0),
        "meta_tokens": nrm(ks[1], (N_META, D_MODEL), 1.0),
        "norm1_g": 1.0 + nrm(ks[2], (DEPTH, D_MODEL), 0.01),
        "w_in": nrm(ks[3], (DEPTH, D_MODEL, D_IN_PROJ), D_MODEL ** -0.5),
        "ml_conv_w": nrm(ks[4], (DEPTH, ML_CONV, 2 * ML_QK_COLS), ML_CONV ** -0.5),
        "ml_conv_b": nrm(ks[5], (DEPTH, 2 * ML_QK_COLS), 0.01),
        "ml_igate_b": nrm(ks[6], (DEPTH, ML_HEADS), 0.1),
        "ml_fgate_b": jnp.linspace(3.0, 6.0, ML_HEADS, dtype=f32)[None, :] + nrm(ks[7], (DEPTH, ML_HEADS), 0.01),
        "da_lq1": nrm(ks[8], (DEPTH, DA_QK_DIM), 0.1),
        "da_lk1": nrm(ks[9], (DEPTH, DA_QK_DIM), 0.1),
        "da_lq2": nrm(ks[10], (DEPTH, DA_QK_DIM), 0.1),
        "da_lk2": nrm(ks[11], (DEPTH, DA_QK_DIM), 0.1),
        "da_qnorm_g": 1.0 + nrm(ks[12], (DEPTH, DA_QK_DIM), 0.01),
        "da_knorm_g": 1.0 + nrm(ks[13], (DEPTH, DA_QK_DIM), 0.01),
        "da_subln_g": 1.0 + nrm(ks[14], (DEPTH, DA_V_DIM), 0.01),
        "ml_outnorm_g": 1.0 + nrm(ks[15], (DEPTH, ML_HEADS, ML_V_DIM), 0.01),
        "w_out": nrm(ks[16], (DEPTH, D_MIX, D_MODEL), D_MIX ** -0.5),
        "norm2_g": 1.0 + nrm(ks[17], (DEPTH, D_MODEL), 0.01),
        "peer_wq": nrm(ks[18], (DEPTH, D_MODEL, PEER_HEADS * PEER_DKEY), D_MODEL ** -0.5),
        "peer_subkeys": nrm(ks[19], (DEPTH, PEER_HEADS, 2, PEER_KEYS, PEER_DKEY // 2), (PEER_DKEY // 2) ** -0.5),
        "peer_u": nrm(ks[20], (DEPTH, PEER_N, D_MODEL), D_MODEL ** -0.5),
        "peer_v": nrm(ks[21], (DEPTH, PEER_N, D_MODEL), 0.25),
    }


def reference(x, meta_tokens, norm1_g, w_in, ml_conv_w, ml_conv_b, ml_igate_b, ml_fgate_b,
              da_lq1, da_lk1, da_lq2, da_lk2, da_qnorm_g, da_knorm_g, da_subln_g,
              ml_outnorm_g, w_out, norm2_g, peer_wq, peer_subkeys, peer_u, peer_v):
    B = x.shape[0]
    meta = jnp.broadcast_to(meta_tokens[None].astype(x.dtype), (B, N_META, x.shape[-1]))
    h = jnp.concatenate([meta, x], axis=1)
    for l in range(DEPTH):
        lambda_init = 0.8 - 0.6 * math.exp(-0.3 * l)
        h = h + token_mixer(rms_norm(h, norm1_g[l]), w_in[l], ml_conv_w[l], ml_conv_b[l],
                            ml_igate_b[l], ml_fgate_b[l], da_lq1[l], da_lk1[l], da_lq2[l], da_lk2[l],
                            da_qnorm_g[l], da_knorm_g[l], da_subln_g[l], ml_outnorm_g[l],
                            w_out[l], lambda_init)
        h = h + peer_ffn(rms_norm(h, norm2_g[l]), peer_wq[l], peer_subkeys[l], peer_u[l], peer_v[l])
    return h[:, N_META:]
```

```python
import numpy as np
import ml_dtypes
from contextlib import ExitStack
import concourse.bass as bass
import concourse.mybir as mybir
from concourse.bass_utils import run_bass_kernel_spmd

F32 = mybir.dt.float32
BF16 = mybir.dt.bfloat16
U32 = mybir.dt.uint32
I32 = mybir.dt.int32
AF = mybir.ActivationFunctionType
ALU = mybir.AluOpType
AX = mybir.AxisListType

ENGS = ("pe", "act", "dve", "pool", "sp")
DMA_ENGS = ("sp", "pool")
SEM_CAP = 6000
NDMA_SEM = 6


class Buf:
    __slots__ = ("name", "w", "r")

    def __init__(self, name=""):
        self.name = name
        self.w = None
        self.r = []


class Op:
    __slots__ = ("eng", "fn", "deps", "id", "is_dma", "need", "sem", "val", "dsem")

    def __init__(self, eng, fn, is_dma):
        self.eng = eng
        self.fn = fn
        self.is_dma = is_dma
        self.deps = []
        self.need = False
        self.sem = None
        self.val = None
        self.dsem = None


class Prog:
    def __init__(self, nc):
        self.nc = nc
        self.ops = []
        self.per_eng = {e: [] for e in ENGS}
        self.dma_rr = {e: 0 for e in DMA_ENGS}
        self.dma_last = {e: [None] * NDMA_SEM for e in DMA_ENGS}
        self.bar = {e: None for e in ENGS}
        self.out_dmas = []
        self.bar_from = 0

    def add(self, eng, fn, reads=(), writes=(), dma=False, extra_deps=()):
        op = Op(eng, fn, dma)
        op.id = len(self.ops)
        deps = set(extra_deps)
        for b in reads:
            if b.w is not None:
                deps.add(b.w)
        for b in writes:
            if b.w is not None:
                deps.add(b.w)
            deps.update(b.r)
        if self.bar[eng] is not None:
            deps.add(self.bar[eng])
            self.bar[eng] = None
        if dma:
            k = self.dma_rr[eng]
            self.dma_rr[eng] = (k + 1) % NDMA_SEM
            op.dsem = k
            prev = self.dma_last[eng][k]
            if prev is not None:
                deps.add(prev)
            self.dma_last[eng][k] = op.id
        deps.discard(op.id)
        op.deps = sorted(deps)
        for b in reads:
            if not dma:
                b.r = [i for i in b.r if not (self.ops[i].eng == eng and not self.ops[i].is_dma)]
            b.r.append(op.id)
        for b in writes:
            b.w = op.id
            b.r = []
        self.ops.append(op)
        self.per_eng[eng].append(op)
        return op.id

    def barrier(self):
        deps = []
        for e in ENGS:
            for o in reversed(self.per_eng[e]):
                if not o.is_dma:
                    deps.append(o.id)
                    break
        deps += [o.id for o in self.ops[self.bar_from:] if o.is_dma]
        bid = self.add("act", lambda e: e.nop(), extra_deps=deps)
        self.bar_from = len(self.ops)
        for e in ENGS:
            self.bar[e] = bid
        self.bar["act"] = None
        return bid

    def emit(self, es):
        nc = self.nc
        ops = self.ops
        for o in ops:
            for d in o.deps:
                ops[d].need = True
        for i in self.out_dmas:
            ops[i].need = True
        sems = {}

        def getsem(key):
            if key not in sems:
                sems[key] = es.enter_context(nc.semaphore("s_%s_%s_%s" % key))
            return sems[key]

        cnt = {e: 0 for e in ENGS}
        dcnt = {(e, k): 0 for e in DMA_ENGS for k in range(NDMA_SEM)}
        for o in ops:
            if o.is_dma:
                c = dcnt[(o.eng, o.dsem)]
                gen = c // (SEM_CAP // 16)
                o.sem = getsem(("d", o.eng, "%d_%d" % (o.dsem, gen)))
                o.val = 16 * (c % (SEM_CAP // 16) + 1)
                dcnt[(o.eng, o.dsem)] = c + 1
            elif o.need:
                c = cnt[o.eng]
                gen = c // SEM_CAP
                o.sem = getsem(("c", o.eng, gen))
                o.val = c % SEM_CAP + 1
                cnt[o.eng] = c + 1
        engmap = {"pe": "tensor", "act": "scalar", "dve": "vector", "pool": "gpsimd", "sp": "sync"}
        blk = es.enter_context(nc.Block())
        for en in ENGS:
            lst = self.per_eng[en]
            final_waits = [ops[i] for i in self.out_dmas] if en == "sp" else []

            def body(e, lst=lst, en=en, final_waits=final_waits):
                waited = {}
                for o in lst:
                    for d in o.deps:
                        s = ops[d]
                        if s.eng == "pe" and en == "pe" and not s.is_dma:
                            continue
                        key = id(s.sem)
                        if waited.get(key, 0) >= s.val and not s.is_dma:
                            continue
                        if s.is_dma and waited.get(key, 0) >= s.val:
                            continue
                        e.wait_ge(s.sem, s.val)
                        waited[key] = max(waited.get(key, 0), s.val)
                    ins = o.fn(e)
                    if o.is_dma:
                        ins.then_inc(o.sem, 16)
                    elif o.need:
                        ins.then_inc(o.sem, 1)
                for s in final_waits:
                    e.wait_ge(s.sem, s.val)

            getattr(blk, engmap[en])(body)


D = 2048
NT = 17
TC = NT * 128
NO = 9
TO = NO * 128
OWN0 = TC - TO
NOWN = 1032
EPS = 1e-6
NEG = -1.0e30
LAMBDA_INIT = 0.2
DIN = 6152
NEXP = 16384
GC = 4

C_ID = 0
C_IOTA = 128
C_MADD = 256
C_TRI = 384
C_ONES = 512
C_SEL = 640
C_AB = 1152
C_PIDX = 1220
NCONST = 1221


def make_consts():
    c = np.zeros((128, NCONST), np.float32)
    p = np.arange(128)[:, None].astype(np.float32)
    t = np.arange(128)[None, :].astype(np.float32)
    c[:, C_ID:C_ID + 128] = (p == t)
    c[:, C_IOTA:C_IOTA + 128] = t
    c[:, C_MADD:C_MADD + 128] = np.where(p <= t, 0.0, 30000.0)
    c[:, C_TRI:C_TRI + 128] = (t >= p)
    c[:, C_ONES:C_ONES + 128] = 1.0
    for h in range(4):
        c[h, C_SEL + h * 128:C_SEL + (h + 1) * 128] = 1.0
    start = 2.0 ** (-8.0 / 4)
    for h in range(4):
        slope = np.float32(start ** (h + 1))
        for d in range(NT):
            c[:, C_AB + h * NT + d] = slope * (p[:, 0] - 128.0 * d)
    c[:, C_PIDX] = p[:, 0]
    return c


class TT:
    def __init__(self, es, nc, name, shape, dtype, psum=False):
        cm = nc.psum_tensor(name, shape, dtype) if psum else nc.sbuf_tensor(name, shape, dtype)
        self.t = es.enter_context(cm)
        self.b = Buf(name)
        self.subs = {}

    def __getitem__(self, k):
        return self.t[k]

    def sub(self, key):
        if key not in self.subs:
            self.subs[key] = Buf("%s/%s" % (self.b.name, key))
        return self.subs[key]


def _mm(out, lhsT, rhs, start, stop):
    return lambda e: e.matmul(out, lhsT, rhs, start=start, stop=stop)


def _tr(out, in_, ident):
    return lambda e: e.transpose(out, in_, ident)


def _act(out, in_, func, bias=None, scale=None, accum_out=None):
    kw = {}
    if bias is not None:
        kw["bias"] = bias
    if scale is not None:
        kw["scale"] = scale
    if accum_out is not None:
        kw["accum_out"] = accum_out
    return lambda e: e.activation(out, in_, func, **kw)


def _cp(out, in_):
    return lambda e: e.tensor_copy(out, in_)


def _tt(out, in0, in1, op):
    return lambda e: e.tensor_tensor(out, in0, in1, op)


def _ts(out, in0, s1, s2, op0, op1=None):
    if op1 is None:
        return lambda e: e.tensor_scalar(out, in0, s1, None, op0)
    return lambda e: e.tensor_scalar(out, in0, s1, s2, op0, op1)


def _stt(out, in0, scalar, in1, op0, op1):
    return lambda e: e.scalar_tensor_tensor(out, in0, scalar, in1, op0, op1)


def _dma(out, in_):
    return lambda e: e.dma_start(out=out, in_=in_)


def _recip(out, in_):
    return lambda e: e.reciprocal(out, in_)


def _memset(ap, v):
    return lambda e: e.memset(ap, v)


CTX_BLKS = [(0, 512), (512, 512), (1024, 512), (1536, 512), (2048, 128)]
OWN_BLKS = [(1024, 512), (1536, 512), (2048, 128)]


def build(stop_after=None, dbg=()):
    nc = bass.Bass("TRN2", target_bir_lowering=False)
    P = Prog(nc)
    dbg = set(dbg)

    def din(name, shape, dt=F32):
        return nc.dram_tensor(name, shape, dt, kind="ExternalInput").ap()

    def dscr(name, shape, dt):
        kind = "ExternalOutput" if name in dbg else "Internal"
        return nc.dram_tensor(name, shape, dt, kind=kind).ap()

    xc = din("xc", [TC, D])
    nmask = din("nmask", [TC])
    valid = din("valid", [TC])
    consts = din("consts", [128, NCONST])
    norm1_g = din("norm1_g", [D])
    w_in = din("w_in", [D, DIN])
    conv_w = din("ml_conv_w", [4, 1024])
    conv_b = din("ml_conv_b", [1024])
    igate_b = din("ml_igate_b", [4])
    fgate_b = din("ml_fgate_b", [4])
    lq1 = din("da_lq1", [128])
    lk1 = din("da_lk1", [128])
    lq2 = din("da_lq2", [128])
    lk2 = din("da_lk2", [128])
    qng = din("da_qnorm_g", [128])
    kng = din("da_knorm_g", [128])
    subln_g = din("da_subln_g", [256])
    outnorm_g = din("ml_outnorm_g", [4, 256])
    w_out = din("w_out", [D, D])
    norm2_g = din("norm2_g", [D])
    wq = din("peer_wq", [D, D])
    subkeys = din("peer_subkeys", [16, 128, 128])
    pu = din("peer_u", [NEXP, D])
    pv = din("peer_v", [NEXP, D])
    yo = nc.dram_tensor("yo", [TO, D], F32, kind="ExternalOutput").ap()

    qT_d = dscr("qT_d", [8, 128, TO], BF16)
    kT_d = dscr("kT_d", [8, 128, TC], BF16)
    mlqk_d = dscr("mlqk_d", [8, 128, TC], F32)
    v_d = dscr("v_d", [TC, 1024], BF16)
    mlv_d = dscr("mlv_d", [TC, 1024], BF16)
    mix_d = dscr("mix_d", [TO, D], BF16)
    sigo_d = dscr("sigo_d", [TO, 1024], BF16)
    h1_d = dscr("h1_d", [TO, D], F32)
    R_d = dscr("R_d", [TO, 128, 128], BF16)
    GT_d = dscr("GT_d", [128, 128, TO], BF16)
    act_d = dscr("act_d", [NEXP // 128, 128, NOWN], BF16)
    dbg_d = {}

    def dbgout(name, shape, dt=F32):
        dbg_d[name] = nc.dram_tensor("dbg_" + name, shape, dt, kind="ExternalOutput").ap()
        return dbg_d[name]

    es = ExitStack()
    with es:
        ncd = es.enter_context(nc.allow_non_contiguous_dma(reason="small strided param loads"))
        cst = TT(es, nc, "cst", [128, NCONST], F32)
        identb = TT(es, nc, "identb", [128, 128], BF16)
        onesb = TT(es, nc, "onesb", [128, 128], BF16)
        P.add("sp", _dma(cst[:], consts), writes=[cst.b], dma=True)
        P.add("dve", _cp(identb[:], cst[:, C_ID:C_ID + 128]), reads=[cst.b], writes=[identb.b])
        P.add("dve", _cp(onesb[:], cst[:, C_ONES:C_ONES + 128]), reads=[cst.b], writes=[onesb.b])
        ident_f = cst[:, C_ID:C_ID + 128]

        DK_S = 128.0 ** -0.5
        gq = TT(es, nc, "gq", [128, 1], F32)
        gk = TT(es, nc, "gk", [128, 1], F32)
        P.add("sp", _dma(gq[:], qng.rearrange("(p o) -> p o", o=1)), writes=[gq.b], dma=True)
        P.add("sp", _dma(gk[:], kng.rearrange("(p o) -> p o", o=1)), writes=[gk.b], dma=True)
        P.add("dve", _ts(gq[:], gq[:], DK_S, None, ALU.mult), reads=[gq.b], writes=[gq.b])
        st_b = es.enter_context(ExitStack())
        urep = [TT(st_b, nc, "urep%d" % h, [128, TC], F32) for h in range(4)]
        acen = TT(st_b, nc, "acen", [128, NT, 8], F32)
        st_a = es.enter_context(ExitStack())
        xnT = TT(st_a, nc, "xnT", [128, 16, TC], BF16)

        s1 = es.enter_context(ExitStack())
        if True:
            g1b = TT(s1, nc, "g1b", [128, D], F32)
            P.add("sp", _dma(g1b[:], norm1_g.partition_broadcast(128)), writes=[g1b.b], dma=True)
            xts = [TT(s1, nc, "xt%d" % i, [128, D], F32) for i in range(2)]
            xnbs = [TT(s1, nc, "xnb%d" % i, [128, D], BF16) for i in range(2)]
            junk = TT(s1, nc, "junk", [128, D], BF16)
            ssq = [TT(s1, nc, "ssq%d" % i, [128, 1], F32) for i in range(2)]
            rsq = [TT(s1, nc, "rsq%d" % i, [128, 1], F32) for i in range(2)]
            ptr = [TT(s1, nc, "ptr%d" % i, [128, 8, 128], BF16, psum=True) for i in range(2)]
            def s1_tile(i):
                xt, xnb, ss, rs = xts[i % 2], xnbs[i % 2], ssq[i % 2], rsq[i % 2]
                P.add("sp", _dma(xt[:], xc[i * 128:(i + 1) * 128, :]), writes=[xt.b], dma=True)
                P.add("act", _act(junk[:], xt[:], AF.Square, accum_out=ss[:]), reads=[xt.b], writes=[junk.b, ss.b])
                P.add("dve", _ts(rs[:], ss[:], 1.0 / D, EPS, ALU.mult, ALU.add), reads=[ss.b], writes=[rs.b])
                P.add("act", _act(rs[:], rs[:], AF.Sqrt), reads=[rs.b], writes=[rs.b])
                P.add("dve", _recip(rs[:], rs[:]), reads=[rs.b], writes=[rs.b])
                P.add("dve", _stt(xnb[:], xt[:], rs[:, 0:1], g1b[:], ALU.mult, ALU.mult),
                      reads=[xt.b, rs.b, g1b.b], writes=[xnb.b])
                for half in range(2):
                    ps = ptr[half]
                    for j in range(8):
                        kt = half * 8 + j
                        P.add("pe", _tr(ps[:, j, :], xnb[:, kt * 128:(kt + 1) * 128], identb[:]),
                              reads=[xnb.b, identb.b], writes=[ps.b])
                    P.add("act" if half == 0 else "dve",
                          (_act(xnT[:, half * 8:(half + 1) * 8, i * 128:(i + 1) * 128], ps[:], AF.Copy) if half == 0
                           else _cp(xnT[:, half * 8:(half + 1) * 8, i * 128:(i + 1) * 128], ps[:])),
                          reads=[ps.b], writes=[xnT.sub(i)])
            for i in range(NT):
                s1_tile(i)
        if "xnT" in dbg:
            o = dbgout("xnT", [128, 16 * TC], BF16)
            P.out_dmas.append(P.add("sp", _dma(o, xnT[:].rearrange("p k t -> p (k t)")), reads=[xnT.sub(i) for i in range(NT)], dma=True))
        if stop_after == "S1":
            P.barrier()
            return _finish(nc, P, es, yo)

        with ExitStack() as s2:
            wts = [TT(s2, nc, "wt%d" % i, [128, 16, 512], BF16) for i in range(2)]
            pmm = [TT(s2, nc, "pmm%d" % i, [128, 512], F32, psum=True) for i in range(4)]
            pss = [TT(s2, nc, "pss%d" % i, [128, 512], F32, psum=True) for i in range(2)]
            sqs = [TT(s2, nc, "sq%d" % i, [128, 512], BF16) for i in range(2)]
            rss = [TT(s2, nc, "rs%d" % i, [128, 512], F32) for i in range(2)]
            qns = [TT(s2, nc, "qn%d" % i, [128, 512], BF16) for i in range(2)]
            rws = [TT(s2, nc, "rw%d" % i, [128, 512], F32) for i in range(2)]
            vts = [TT(s2, nc, "vt%d" % i, [128, 512], BF16) for i in range(2)]
            cnt = {"mm": 0, "e": 0}

            def xn_bufs(n0, N):
                return [xnT.sub(i) for i in range(n0 // 128, (n0 + N) // 128)]

            def load_w(it_):
                wt = wts[it_ % 2]
                gc_ = order[it_]
                src = w_in[:, gc_ * 512:(gc_ + 1) * 512].rearrange("(kt p) c -> p kt c", p=128)
                P.add("pool", _dma(wt[:], src), writes=[wt.b], dma=True)

            groups = ([("qn", qT_d, True, gq)] * 2 + [("qn", kT_d, False, gk)] * 2 +
                      [("tm", v_d, False, AF.Copy)] * 2 + [("raw", mlqk_d, False, None)] * 2 +
                      [("tm", mlv_d, False, AF.Copy)] * 2 + [("tm", sigo_d, True, AF.Sigmoid)] * 2)
            order = [4, 5, 8, 9, 2, 3, 6, 7, 0, 1, 10, 11]
            load_w(0)
            for it_, g in enumerate(order):
                kind, dst, own, extra = groups[g]
                if it_ + 1 < len(order):
                    load_w(it_ + 1)
                wt = wts[it_ % 2]
                gl = g % 2
                if kind in ("qn", "raw"):
                    blks = OWN_BLKS if own else CTX_BLKS
                    for sub in range(4):
                        cb = gl * 4 + sub
                        for (n0, N) in blks:
                            ps = pmm[cnt["mm"] % 4]
                            cnt["mm"] += 1
                            for kt in range(16):
                                P.add("pe", _mm(ps[:, :N], wt[:, kt, sub * 128:(sub + 1) * 128], xnT[:, kt, n0:n0 + N],
                                                kt == 0, kt == 15), reads=[wt.b] + xn_bufs(n0, N), writes=[ps.b])
                            d0 = n0 - (OWN0 if own else 0)
                            k = cnt["e"] % 2
                            cnt["e"] += 1
                            if kind == "raw":
                                rw = rws[k]
                                P.add("act", _act(rw[:, :N], ps[:, :N], AF.Copy), reads=[ps.b], writes=[rw.b])
                                P.add("sp", _dma(dst[cb, :, d0:d0 + N], rw[:, :N]), reads=[rw.b], dma=True)
                            else:
                                sq, rs, qn, pz = sqs[k], rss[k], qns[k], pss[k]
                                P.add("act", _act(sq[:, :N], ps[:, :N], AF.Square), reads=[ps.b], writes=[sq.b])
                                P.add("pe", _mm(pz[:, :N], onesb[:], sq[:, :N], True, True), reads=[onesb.b, sq.b], writes=[pz.b])
                                P.add("dve", _ts(rs[:, :N], pz[:, :N], 1.0 / 128, EPS, ALU.mult, ALU.add), reads=[pz.b], writes=[rs.b])
                                P.add("act", _act(rs[:, :N], rs[:, :N], AF.Ln), reads=[rs.b], writes=[rs.b])
                                P.add("act", _act(rs[:, :N], rs[:, :N], AF.Exp, scale=-0.5), reads=[rs.b], writes=[rs.b])
                                P.add("dve", _stt(qn[:, :N], ps[:, :N], extra[:, 0:1], rs[:, :N], ALU.mult, ALU.mult),
                                      reads=[ps.b, extra.b, rs.b], writes=[qn.b])
                                P.add("sp", _dma(dst[cb, :, d0:d0 + N], qn[:, :N]), reads=[qn.b], dma=True)
                else:
                    tiles = range(8, NT) if own else range(NT)
                    for i in tiles:
                        ps = pmm[cnt["mm"] % 4]
                        cnt["mm"] += 1
                        for kt in range(16):
                            P.add("pe", _mm(ps[:], xnT[:, kt, i * 128:(i + 1) * 128], wt[:, kt, :], kt == 0, kt == 15),
                                  reads=[wt.b, xnT.sub(i)], writes=[ps.b])
                        k = cnt["e"] % 2
                        cnt["e"] += 1
                        vt = vts[k]
                        P.add("act", _act(vt[:], ps[:], extra), reads=[ps.b], writes=[vt.b])
                        r0 = (i - 8 if own else i) * 128
                        P.add("sp", _dma(dst[r0:r0 + 128, gl * 512:(gl + 1) * 512], vt[:]), reads=[vt.b], dma=True)
        P.barrier()
        s1.close()
        if stop_after == "S2":
            return _finish(nc, P, es, yo)

        with ExitStack() as s3:
            wg = TT(s3, nc, "wg", [128, 16, 8], BF16)
            P.add("pool", _dma(wg[:], w_in[:, 6144:6152].rearrange("(kt p) c -> p kt c", p=128)), writes=[wg.b], dma=True)
            nm4 = TT(s3, nc, "nm4", [4, TC], F32)
            va4 = TT(s3, nc, "va4", [4, TC], F32)
            P.add("sp", _dma(nm4[:], nmask.partition_broadcast(4)), writes=[nm4.b], dma=True)
            P.add("sp", _dma(va4[:], valid.partition_broadcast(4)), writes=[va4.b], dma=True)
            ib = TT(s3, nc, "ib", [4, 1], F32)
            fbn = TT(s3, nc, "fbn", [4, 1], F32)
            P.add("sp", _dma(ib[:], igate_b.rearrange("(p o) -> p o", o=1)), writes=[ib.b], dma=True)
            P.add("sp", _dma(fbn[:], fgate_b.rearrange("(p o) -> p o", o=1)), writes=[fbn.b], dma=True)
            P.add("dve", _ts(fbn[:], fbn[:], -1.0, None, ALU.mult), reads=[fbn.b], writes=[fbn.b])
            ones4 = TT(s3, nc, "ones4", [4, TC], F32)
            P.add("dve", _memset(ones4[:], 1.0), writes=[ones4.b])
            g_li = TT(s3, nc, "g_li", [4, TC], F32)
            g_lf = TT(s3, nc, "g_lf", [4, TC], F32)
            g_B = TT(s3, nc, "g_B", [4, TC], F32)
            g_A = TT(s3, nc, "g_A", [4, TC], F32)
            g_u = TT(s3, nc, "g_u", [4, TC], F32)
            g_en = TT(s3, nc, "g_en", [4, TC], F32)
            g_t = [TT(s3, nc, "g_t%d" % i, [4, 512], F32) for i in range(2)]
            pli = TT(s3, nc, "pli", [128, 512], F32, psum=True)
            plf = TT(s3, nc, "plf", [128, 512], F32, psum=True)
            pur = [TT(s3, nc, "pur%d" % i, [128, 512], F32, psum=True) for i in range(2)]
            pac = TT(s3, nc, "pac", [128, NT, 8], F32, psum=True)
            for bi, (n0, N) in enumerate(CTX_BLKS):
                xb = [xnT.sub(i) for i in range(n0 // 128, (n0 + N) // 128)]
                for kt in range(16):
                    P.add("pe", _mm(pli[0:4, :N], wg[:, kt, 0:4], xnT[:, kt, n0:n0 + N], kt == 0, kt == 15),
                          reads=[wg.b] + xb, writes=[pli.b])
                for kt in range(16):
                    P.add("pe", _mm(plf[0:4, :N], wg[:, kt, 4:8], xnT[:, kt, n0:n0 + N], kt == 0, kt == 15),
                          reads=[wg.b] + xb, writes=[plf.b])
                P.add("dve", _stt(g_li[:, n0:n0 + N], pli[0:4, :N], ib[:, 0:1], nm4[:, n0:n0 + N], ALU.add, ALU.add),
                      reads=[pli.b, ib.b, nm4.b], writes=[g_li.sub(bi)])
                t0, t1 = g_t
                P.add("act", _act(t0[:, :N], plf[0:4, :N], AF.Exp, bias=fbn[:, 0:1], scale=-1.0),
                      reads=[plf.b, fbn.b], writes=[t0.b])
                P.add("act", _act(t1[:, :N], t0[:, :N], AF.Ln, bias=1.0), reads=[t0.b], writes=[t1.b])
                P.add("dve", _stt(g_lf[:, n0:n0 + N], t1[:, :N], -1.0, va4[:, n0:n0 + N], ALU.mult, ALU.mult),
                      reads=[t1.b, va4.b], writes=[g_lf.sub(bi)])
            lib = [g_li.sub(i) for i in range(5)]
            lfb = [g_lf.sub(i) for i in range(5)]
            if stop_after == "S3a":
                P.barrier()
                return _finish(nc, P, es, yo)
            P.add("dve", lambda e: e.tensor_tensor_scan(g_B[:], ones4[:], g_lf[:], 0.0, ALU.mult, ALU.add),
                  reads=[ones4.b] + lfb, writes=[g_B.b])
            P.add("dve", _tt(g_A[:], g_li[:], g_B[:], ALU.subtract), reads=lib + [g_B.b], writes=[g_A.b])
            P.add("dve", lambda e: e.tensor_tensor_scan(g_u[:], ones4[:], g_A[:], 0.0, ALU.mult, ALU.max),
                  reads=[ones4.b, g_A.b], writes=[g_u.b])
            P.add("dve", _tt(g_en[:], g_u[:], g_B[:], ALU.add), reads=[g_u.b, g_B.b], writes=[g_en.b])
            P.add("act", _act(g_en[:], g_en[:], AF.Exp, scale=-2.0), reads=[g_en.b], writes=[g_en.b])
            if stop_after == "S3b":
                P.barrier()
                return _finish(nc, P, es, yo)
            k = 0
            for h in range(4):
                for (n0, N) in CTX_BLKS:
                    ps = pur[k % 2]
                    k += 1
                    P.add("pe", _mm(ps[:, :N], cst[0:4, C_SEL + h * 128:C_SEL + (h + 1) * 128], g_u[:, n0:n0 + N], True, True),
                          reads=[cst.b, g_u.b], writes=[ps.b])
                    P.add("act", _act(urep[h][:, n0:n0 + N], ps[:, :N], AF.Copy), reads=[ps.b], writes=[urep[h].b])
            if stop_after == "S3c":
                P.barrier()
                return _finish(nc, P, es, yo)
            for i in range(NT):
                P.add("pe", _mm(pac[:, i, 0:4], g_A[:, i * 128:(i + 1) * 128], cst[0:4, C_ID:C_ID + 4], True, True),
                      reads=[g_A.b, cst.b], writes=[pac.b])
                P.add("pe", _mm(pac[:, i, 4:8], g_en[:, i * 128:(i + 1) * 128], cst[0:4, C_ID:C_ID + 4], True, True),
                      reads=[g_en.b, cst.b], writes=[pac.b])
            P.add("act", _act(acen[:], pac[:], AF.Copy), reads=[pac.b], writes=[acen.b])
            if "gates" in dbg:
                o = dbgout("gates", [4, 4 * TC])
                for j, gt_ in enumerate((g_li, g_lf, g_u, g_en)):
                    P.out_dmas.append(P.add("sp", _dma(o[:, j * TC:(j + 1) * TC], gt_[:]),
                                            reads=[gt_.b] + [gt_.sub(i) for i in range(5)], dma=True))
        P.barrier()
        st_a.close()
        if stop_after == "S3":
            return _finish(nc, P, es, yo)
        st_c = es.enter_context(ExitStack())
        mqT = [TT(st_c, nc, "mqT%d" % h, [128, TC], BF16) for h in range(4)]
        mkT = [TT(st_c, nc, "mkT%d" % h, [128, TC], BF16) for h in range(4)]
        mk_tok = TT(st_c, nc, "mk_tok", [128, NT, 4, 128], BF16)

        with ExitStack() as s4:
            cw = TT(s4, nc, "cw", [128, 4, 8], F32)
            cbias = TT(s4, nc, "cbias", [128, 8], F32)
            for j in range(4):
                P.add("sp", _dma(cw[:, j, :], conv_w[j].rearrange("(cb p) -> p cb", p=128)), writes=[cw.sub(j)], dma=True)
            P.add("sp", _dma(cbias[:], conv_b.rearrange("(cb p) -> p cb", p=128)), writes=[cbias.b], dma=True)
            xcs = [TT(s4, nc, "xcv%d" % i, [128, 3 + TC], F32) for i in range(2)]
            accs = [TT(s4, nc, "acc%d" % i, [128, TC], F32) for i in range(2)]
            ptk = [TT(s4, nc, "ptk%d" % i, [128, 4, 128], BF16, psum=True) for i in range(2)]
            for xv in xcs:
                P.add("dve", _memset(xv[:, 0:3], 0.0), writes=[xv.sub("pad")])
            for cb in range(8):
                xv, acc = xcs[cb % 2], accs[cb % 2]
                P.add("sp", _dma(xv[:, 3:3 + TC], mlqk_d[cb]), writes=[xv.b], dma=True)
                rd = [xv.b, xv.sub("pad"), cbias.b] + [cw.sub(j) for j in range(4)]
                P.add("dve", _ts(acc[:], xv[:, 0:TC], cw[:, 0, cb:cb + 1], cbias[:, cb:cb + 1], ALU.mult, ALU.add),
                      reads=rd, writes=[acc.b])
                for j in range(1, 4):
                    P.add("dve", _stt(acc[:], xv[:, j:j + TC], cw[:, j, cb:cb + 1], acc[:], ALU.mult, ALU.add),
                          reads=rd + [acc.b], writes=[acc.b])
                if cb < 4:
                    P.add("act", _act(mqT[cb][:], acc[:], AF.Silu), reads=[acc.b], writes=[mqT[cb].b])
                else:
                    P.add("act", _act(acc[:], acc[:], AF.Silu), reads=[acc.b], writes=[acc.b])
                    P.add("dve", _ts(mkT[cb - 4][:], acc[:], DK_S, None, ALU.mult), reads=[acc.b], writes=[mkT[cb - 4].b])
            for i in range(NT):
                ps = ptk[i % 2]
                for h in range(4):
                    P.add("pe", _tr(ps[:, h, :], mkT[h][:, i * 128:(i + 1) * 128], identb[:]),
                          reads=[mkT[h].b, identb.b], writes=[ps.b])
                P.add("act" if i % 2 == 0 else "dve",
                      _act(mk_tok[:, i, :, :], ps[:], AF.Copy) if i % 2 == 0 else _cp(mk_tok[:, i, :, :], ps[:]),
                      reads=[ps.b], writes=[mk_tok.b])
            if "s4" in dbg:
                o = dbgout("mqk", [128, 8 * TC], BF16)
                for h in range(4):
                    P.out_dmas.append(P.add("sp", _dma(o[:, h * TC:(h + 1) * TC], mqT[h][:]), reads=[mqT[h].b], dma=True))
                    P.out_dmas.append(P.add("sp", _dma(o[:, (4 + h) * TC:(5 + h) * TC], mkT[h][:]), reads=[mkT[h].b], dma=True))
                o2 = dbgout("mktok", [128, NT * 4 * 128], BF16)
                P.out_dmas.append(P.add("sp", _dma(o2, mk_tok[:].rearrange("p a b c -> p (a b c)")), reads=[mk_tok.b], dma=True))
                o3 = dbgout("urep", [4, TC])
                for h in range(4):
                    P.out_dmas.append(P.add("sp", _dma(o3[h:h + 1, :], urep[h][5:6, :]), reads=[urep[h].b], dma=True))
                o4 = dbgout("acen", [128, NT * 8])
                P.out_dmas.append(P.add("sp", _dma(o4, acen[:].rearrange("p a b -> p (a b)")), reads=[acen.b], dma=True))
        P.barrier()
        if stop_after == "S4":
            return _finish(nc, P, es, yo)

        madd = cst[:, C_MADD:C_MADD + 128]

        def rr(gens):
            gens = list(gens)
            while gens:
                for g_ in list(gens):
                    try:
                        next(g_)
                    except StopIteration:
                        gens.remove(g_)

        with ExitStack() as s5:
            zcol = TT(s5, nc, "zcol", [128, 1], F32)
            P.add("dve", _memset(zcol[:], 0.0), writes=[zcol.b])
            vtl = [TT(s5, nc, "vtl%d" % i, [128, 4, 257], BF16) for i in range(2)]
            for vt in vtl:
                P.add("dve", _memset(vt[:, :, 256:257], 1.0), writes=[vt.sub("ones")])
            Cf = [TT(s5, nc, "Cf%d" % h, [128, 257], F32) for h in range(4)]
            Cb = [TT(s5, nc, "Cb%d" % h, [128, 257], BF16) for h in range(4)]
            for h in range(4):
                P.add("dve", _memset(Cf[h][:], 0.0), writes=[Cf[h].b])
                P.add("dve", _memset(Cb[h][:], 0.0), writes=[Cb[h].b])
            gout = TT(s5, nc, "gout", [128, 4, 256], F32)
            P.add("sp", _dma(gout[:].rearrange("p a b -> p (a b)"), outnorm_g.rearrange("a b -> (a b)").partition_broadcast(128)),
                  writes=[gout.b], dma=True)
            sigs = [TT(s5, nc, "sig%d" % i, [128, 1024], BF16) for i in range(2)]
            mouts = [TT(s5, nc, "mout%d" % i, [128, 1024], BF16) for i in range(2)]
            R4 = range(4)
            dms = [TT(s5, nc, "dm%d" % i, [128, 128], F32) for i in R4]
            wts_ = [TT(s5, nc, "wtd%d" % i, [128, 128], F32) for i in R4]
            Sts = [TT(s5, nc, "St%d" % i, [128, 128], BF16) for i in R4]
            wints = [TT(s5, nc, "wint%d" % i, [128, 128], F32) for i in R4]
            qss = [TT(s5, nc, "qs%d" % i, [128, 128], BF16) for i in R4]
            vws = [TT(s5, nc, "vw%d" % i, [128, 257], BF16) for i in R4]
            hms = [TT(s5, nc, "hm%d" % i, [128, 256], F32) for i in R4]
            t1s = [TT(s5, nc, "t1%d" % i, [128, 256], F32) for i in R4]
            jk5s = [TT(s5, nc, "jk5%d" % i, [128, 256], BF16) for i in R4]
            cols = [TT(s5, nc, "col%d" % i, [128, 8], F32) for i in R4]
            snp = [TT(s5, nc, "snp%d" % i, [128, 512], F32, psum=True) for i in R4]
            dcp = [TT(s5, nc, "dcp%d" % i, [128, 512], F32, psum=True) for i in R4]

            def unit(c, h, vt, own, sg_, mo_):
                cs = slice(c * 128, (c + 1) * 128)
                ur = urep[h]
                uprev = zcol[:, 0:1] if c == 0 else ur[:, c * 128 - 1:c * 128]
                uL = ur[:, (c + 1) * 128 - 1:(c + 1) * 128]
                dm, wd, St, wint, qs, vw, hm, t1, col, jk5 = (dms[h], wts_[h], Sts[h], wints[h], qss[h], vws[h], hms[h],
                                                              t1s[h], cols[h], jk5s[h])
                bk, dp_ = snp[h], dcp[h]
                sp_ = bk[:, 0:128]
                np_ = bk[:, 128:385]
                P.add("dve", _stt(dm[:], ur[:, cs], acen[:, c, h:h + 1], madd, ALU.subtract, ALU.add),
                      reads=[ur.b, acen.b, cst.b], writes=[dm.b])
                yield
                P.add("act", _act(wd[:], dm[:], AF.Exp, scale=-1.0), reads=[dm.b], writes=[wd.b])
                yield
                P.add("pe", _mm(sp_, mkT[h][:, cs], mqT[h][:, cs], True, True),
                      reads=[mkT[h].b, mqT[h].b], writes=[bk.sub("s")])
                yield
                P.add("dve", _tt(St[:], sp_, wd[:], ALU.mult), reads=[bk.sub("s"), wd.b], writes=[St.b])
                yield
                P.add("act", _act(wint[:], ur[:, cs], AF.Exp, bias=uprev, scale=-1.0), reads=[ur.b, zcol.b], writes=[wint.b])
                yield
                P.add("dve", _tt(qs[:], mqT[h][:, cs], wint[:], ALU.mult), reads=[mqT[h].b, wint.b], writes=[qs.b])
                yield
                P.add("pe", _mm(np_, St[:], vt[:, h, :], True, False),
                      reads=[St.b, vt.b, vt.sub("ones")], writes=[bk.sub("n")])
                P.add("pe", _mm(np_, qs[:], Cb[h][:], False, True), reads=[qs.b, Cb[h].b], writes=[bk.sub("n")])
                yield
                if own:
                    den = bk[:, 384:385]
                    P.add("act", _act(col[:, 0:1], den, AF.Square), reads=[bk.sub("n")], writes=[col.sub(0)])
                    yield
                    P.add("dve", _ts(col[:, 0:1], col[:, 0:1], acen[:, c, 4 + h:5 + h], None, ALU.max),
                          reads=[col.sub(0), acen.b], writes=[col.sub(0)])
                    yield
                    P.add("act", _act(col[:, 1:2], col[:, 0:1], AF.Ln), reads=[col.sub(0)], writes=[col.sub(1)])
                    yield
                    P.add("act", _act(col[:, 1:2], col[:, 1:2], AF.Exp, scale=-0.5), reads=[col.sub(1)], writes=[col.sub(1)])
                    yield
                    P.add("act", _act(hm[:], bk[:, 128:384], AF.Copy, scale=col[:, 1:2]), reads=[bk.sub("n"), col.sub(1)], writes=[hm.b])
                    yield
                    P.add("act", _act(jk5[:], hm[:], AF.Square, accum_out=col[:, 2:3]), reads=[hm.b], writes=[jk5.b, col.sub(2)])
                    yield
                    P.add("dve", _ts(col[:, 3:4], col[:, 2:3], 1.0 / 256, EPS, ALU.mult, ALU.add), reads=[col.sub(2)], writes=[col.sub(3)])
                    yield
                    P.add("act", _act(col[:, 3:4], col[:, 3:4], AF.Ln), reads=[col.sub(3)], writes=[col.sub(3)])
                    yield
                    P.add("act", _act(col[:, 3:4], col[:, 3:4], AF.Exp, scale=-0.5), reads=[col.sub(3)], writes=[col.sub(3)])
                    yield
                    P.add("dve", _stt(t1[:], hm[:], col[:, 3:4], gout[:, h, :], ALU.mult, ALU.mult),
                          reads=[hm.b, col.sub(3), gout.b], writes=[t1.b])
                    yield
                    P.add("dve", _tt(mo_[:, h * 256:(h + 1) * 256], t1[:], sg_[:, h * 256:(h + 1) * 256], ALU.mult),
                          reads=[t1.b, sg_.b], writes=[mo_.sub(h)])
                    yield
                if c < NT - 1:
                    P.add("dve", _tt(col[:, 4:5], uL, acen[:, c, h:h + 1], ALU.subtract), reads=[ur.b, acen.b], writes=[col.sub(4)])
                    yield
                    P.add("act", _act(col[:, 4:5], col[:, 4:5], AF.Exp, scale=-1.0), reads=[col.sub(4)], writes=[col.sub(4)])
                    yield
                    P.add("dve", _ts(vw[:], vt[:, h, :], col[:, 4:5], None, ALU.mult),
                          reads=[vt.b, vt.sub("ones"), col.sub(4)], writes=[vw.b])
                    yield
                    P.add("pe", _mm(dp_[:, 0:257], mk_tok[:, c, h, :], vw[:], True, True), reads=[mk_tok.b, vw.b], writes=[dp_.b])
                    yield
                    P.add("act", _act(col[:, 5:6], uL, AF.Exp, bias=uprev, scale=-1.0), reads=[ur.b, zcol.b], writes=[col.sub(5)])
                    yield
                    P.add("dve", _stt(Cf[h][:], Cf[h][:], col[:, 5:6], dp_[:, 0:257], ALU.mult, ALU.add),
                          reads=[Cf[h].b, col.sub(5), dp_.b], writes=[Cf[h].b])
                    yield
                    P.add("act", _act(Cb[h][:], Cf[h][:], AF.Copy), reads=[Cf[h].b], writes=[Cb[h].b])
                    yield

            for c in range(NT):
                vt = vtl[c % 2]
                P.add("sp", _dma(vt[:, :, 0:256], mlv_d[c * 128:(c + 1) * 128, :].rearrange("p (h e) -> p h e", h=4)),
                      writes=[vt.b], dma=True)
                own = c >= 8
                sg_ = mo_ = None
                if own:
                    sg_, mo_ = sigs[c % 2], mouts[c % 2]
                    P.add("sp", _dma(sg_[:], sigo_d[(c - 8) * 128:(c - 7) * 128, :]), writes=[sg_.b], dma=True)
                rr([unit(c, h, vt, own, sg_, mo_) for h in range(4)])
                if own:
                    P.add("sp", _dma(mix_d[(c - 8) * 128:(c - 7) * 128, 1024:2048], mo_[:]),
                          reads=[mo_.sub(h) for h in range(4)], dma=True)
        P.barrier()
        st_c.close()
        st_b.close()
        if stop_after == "S5":
            return _finish(nc, P, es, yo)

        with ExitStack() as s6:
            nmcol = TT(s6, nc, "nmcol", [128, NT], F32)
            P.add("sp", _dma(nmcol[:], nmask.rearrange("(t p) -> p t", p=128)), writes=[nmcol.b], dma=True)
            btab = TT(s6, nc, "btab", [128, 4, NT, NT], F32)
            for h in range(4):
                for kj in range(NT):
                    P.add("dve", _ts(btab[:, h, kj, :], cst[:, C_AB + h * NT:C_AB + (h + 1) * NT], nmcol[:, kj:kj + 1], None, ALU.add),
                          reads=[cst.b, nmcol.b], writes=[btab.b])
            tri2 = TT(s6, nc, "tri2", [128, 2, 128], BF16)
            for m_ in range(2):
                P.add("dve", _cp(tri2[:, m_, :], cst[:, C_TRI:C_TRI + 128]), reads=[cst.b], writes=[tri2.b])
            l4 = TT(s6, nc, "l4", [128, 4, 128], F32)
            for j, src in enumerate((lq1, lk1, lq2, lk2)):
                P.add("sp", _dma(l4[:, j, :], src.partition_broadcast(128)), writes=[l4.sub(j)], dma=True)
            lcol = TT(s6, nc, "lcol", [128, 8], F32)
            lpr = TT(s6, nc, "lpr", [128, 2, 128], F32)
            P.add("dve", _tt(lpr[:, 0, :], l4[:, 0, :], l4[:, 1, :], ALU.mult), reads=[l4.sub(0), l4.sub(1)], writes=[lpr.b])
            P.add("dve", _tt(lpr[:, 1, :], l4[:, 2, :], l4[:, 3, :], ALU.mult), reads=[l4.sub(2), l4.sub(3), lpr.b], writes=[lpr.b])
            P.add("dve", lambda e: e.reduce_sum(lcol[:, 0:2], lpr[:], AX.X), reads=[lpr.b], writes=[lcol.b])
            P.add("act", _act(lcol[:, 2:4], lcol[:, 0:2], AF.Exp), reads=[lcol.b], writes=[lcol.b])
            P.add("dve", _tt(lcol[:, 4:5], lcol[:, 3:4], lcol[:, 2:3], ALU.subtract), reads=[lcol.b], writes=[lcol.b])
            P.add("dve", _ts(lcol[:, 5:6], lcol[:, 4:5], -LAMBDA_INIT, None, ALU.add), reads=[lcol.b], writes=[lcol.b])
            neglam = lcol[:, 5:6]
            sgt = TT(s6, nc, "sgt", [128, 256], F32)
            P.add("sp", _dma(sgt[:], subln_g.partition_broadcast(128)), writes=[sgt.b], dma=True)
            P.add("dve", _ts(sgt[:], sgt[:], 1.0 - LAMBDA_INIT, None, ALU.mult), reads=[sgt.b], writes=[sgt.b])
            kTs = [[TT(s6, nc, "kT%d_%d" % (i, m_), [128, TC], BF16) for m_ in range(2)] for i in range(2)]
            qTs = [[TT(s6, nc, "qT%d_%d" % (i, m_), [128, TO], BF16) for m_ in range(2)] for i in range(2)]
            Vs = [TT(s6, nc, "V%d" % i, [128, NT, 257], BF16) for i in range(2)]
            for V in Vs:
                P.add("dve", _memset(V[:, :, 256:257], 1.0), writes=[V.sub("ones")])
            R2 = range(2)
            Pts = [TT(s6, nc, "Pt%d" % i, [128, 2, 128], BF16) for i in range(4)]
            oas = [TT(s6, nc, "oa%d" % i, [128, 256], F32) for i in R2]
            oos = [TT(s6, nc, "oo%d" % i, [128, 256], F32) for i in R2]
            aos = [TT(s6, nc, "ao%d" % i, [128, 256], BF16) for i in R2]
            jk6 = TT(s6, nc, "jk6", [128, 256], BF16)
            col6 = [TT(s6, nc, "c6%d" % i, [128, 8], F32) for i in R2]
            sps = [TT(s6, nc, "sps%d" % i, [128, 512], F32, psum=True) for i in range(4)]
            o0s = [TT(s6, nc, "o0s%d" % i, [128, 512], F32, psum=True) for i in R2]
            o1s = [TT(s6, nc, "o1s%d" % i, [128, 512], F32, psum=True) for i in R2]
            def load_head(h):
                kT, qT, V = kTs[h % 2], qTs[h % 2], Vs[h % 2]
                for m_ in range(2):
                    P.add("sp", _dma(kT[m_][:], kT_d[2 * h + m_]), writes=[kT[m_].b], dma=True)
                    P.add("sp", _dma(qT[m_][:], qT_d[2 * h + m_]), writes=[qT[m_].b], dma=True)
                P.add("sp", _dma(V[:, :, 0:256], v_d[:, h * 256:(h + 1) * 256].rearrange("(t p) c -> p t c", p=128)),
                      writes=[V.b], dma=True)

            def emit_scores(h, b, kj, r):
                kT, qT = kTs[h % 2], qTs[h % 2]
                qi = 8 + b
                sp_, Pt = sps[r], Pts[r]
                for m_ in range(2):
                    P.add("pe", _mm(sp_[:, m_ * 128:(m_ + 1) * 128], kT[m_][:, kj * 128:(kj + 1) * 128],
                                    qT[m_][:, b * 128:(b + 1) * 128], True, True),
                          reads=[kT[m_].b, qT[m_].b], writes=[sp_.b])
                P.add("act", _act(Pt[:].rearrange("p a b -> p (a b)"), sp_[:, 0:256], AF.Exp,
                                  bias=btab[:, h, kj, qi - kj:qi - kj + 1]),
                      reads=[sp_.b, btab.b], writes=[Pt.b])
                if kj == qi:
                    P.add("pool", _tt(Pt[:], Pt[:], tri2[:], ALU.mult), reads=[Pt.b, tri2.b], writes=[Pt.b])

            def emit_av(h, b, kj, r):
                V = Vs[h % 2]
                qi = 8 + b
                rb = (h * NO + b) % 2
                o0, o1 = o0s[rb], o1s[rb]
                Pt = Pts[r]
                P.add("pe", _mm(o0[:, 0:257], Pt[:, 0, :], V[:, kj, :], kj == 0, kj == qi),
                      reads=[Pt.b, V.b, V.sub("ones")], writes=[o0.b])
                P.add("pe", _mm(o1[:, 0:257], Pt[:, 1, :], V[:, kj, :], kj == 0, kj == qi),
                      reads=[Pt.b, V.b, V.sub("ones")], writes=[o1.b])
                if kj == qi:
                    epilogue(h, b)

            def epilogue(h, b):
                rb = (h * NO + b) % 2
                o0, o1 = o0s[rb], o1s[rb]
                c6, oa, oo, ao = col6[rb], oas[rb], oos[rb], aos[rb]
                P.add("dve", _ts(c6[:, 0:1], o0[:, 256:257], 1e-30, None, ALU.max), reads=[o0.b], writes=[c6.sub(0)])
                P.add("dve", _recip(c6[:, 0:1], c6[:, 0:1]), reads=[c6.sub(0)], writes=[c6.sub(0)])
                P.add("dve", _ts(c6[:, 1:2], o1[:, 256:257], 1e-30, None, ALU.max), reads=[o1.b], writes=[c6.sub(1)])
                P.add("dve", _recip(c6[:, 1:2], c6[:, 1:2]), reads=[c6.sub(1)], writes=[c6.sub(1)])
                P.add("dve", _tt(c6[:, 1:2], c6[:, 1:2], neglam, ALU.mult), reads=[c6.sub(1), lcol.b], writes=[c6.sub(1)])
                P.add("act", _act(oa[:], o0[:, 0:256], AF.Copy, scale=c6[:, 0:1]), reads=[o0.b, c6.sub(0)], writes=[oa.b])
                P.add("dve", _stt(oo[:], o1[:, 0:256], c6[:, 1:2], oa[:], ALU.mult, ALU.add),
                      reads=[o1.b, c6.sub(1), oa.b], writes=[oo.b])
                P.add("act", _act(jk6[:], oo[:], AF.Square, accum_out=c6[:, 2:3]), reads=[oo.b], writes=[jk6.b, c6.sub(2)])
                P.add("dve", _ts(c6[:, 3:4], c6[:, 2:3], 1.0 / 256, EPS, ALU.mult, ALU.add), reads=[c6.sub(2)], writes=[c6.sub(3)])
                P.add("act", _act(c6[:, 3:4], c6[:, 3:4], AF.Ln), reads=[c6.sub(3)], writes=[c6.sub(3)])
                P.add("act", _act(c6[:, 3:4], c6[:, 3:4], AF.Exp, scale=-0.5), reads=[c6.sub(3)], writes=[c6.sub(3)])
                P.add("dve", _stt(ao[:], oo[:], c6[:, 3:4], sgt[:], ALU.mult, ALU.mult),
                      reads=[oo.b, c6.sub(3), sgt.b], writes=[ao.b])
                P.add("sp", _dma(mix_d[b * 128:(b + 1) * 128, h * 256:(h + 1) * 256], ao[:]), reads=[ao.b], dma=True)

            steps = [(h, b, kj) for h in range(4) for b in range(NO) for kj in range(8 + b + 1)]
            LOOK = 2
            NR = 4
            load_head(0)
            loaded = {0}
            for i_, (h, b, kj) in enumerate(steps[:LOOK]):
                emit_scores(h, b, kj, i_ % NR)
            for i_, (h, b, kj) in enumerate(steps):
                j_ = i_ + LOOK
                if j_ < len(steps):
                    h2, b2, kj2 = steps[j_]
                    if h2 not in loaded:
                        load_head(h2)
                        loaded.add(h2)
                    emit_scores(h2, b2, kj2, j_ % NR)
                if b == 0 and kj == 0 and h + 1 < 4 and (h + 1) not in loaded:
                    load_head(h + 1)
                    loaded.add(h + 1)
                emit_av(h, b, kj, i_ % NR)
        P.barrier()
        if stop_after == "S6":
            return _finish(nc, P, es, yo)

        OWNB = [(0, 512), (512, 512), (1024, 128)]
        st_x2 = es.enter_context(ExitStack())
        xn2T = TT(st_x2, nc, "xn2T", [128, 16, TO], BF16)
        st_h1 = es.enter_context(ExitStack())
        h1 = TT(st_h1, nc, "h1", [128, NO, D], F32)
        for b in range(NO):
            P.add("sp", _dma(h1[:, b, :], xc[OWN0 + b * 128:OWN0 + (b + 1) * 128, :]), writes=[h1.sub(b)], dma=True)
        with ExitStack() as s7:
            mixT = TT(s7, nc, "mixT", [128, 16, TO], BF16)
            mts = [TT(s7, nc, "mt%d" % i, [128, D], BF16) for i in range(2)]
            wos = [TT(s7, nc, "wo%d" % i, [128, 16, 512], BF16) for i in range(2)]
            ptr7 = [TT(s7, nc, "ptr7%d" % i, [128, 8, 128], BF16, psum=True) for i in range(2)]
            pm7 = [TT(s7, nc, "pm7%d" % i, [128, 512], F32, psum=True) for i in range(3)]

            def load_wo(g):
                P.add("pool", _dma(wos[g % 2][:], w_out[:, g * 512:(g + 1) * 512].rearrange("(kt p) c -> p kt c", p=128)),
                      writes=[wos[g % 2].b], dma=True)

            load_wo(0)
            for b in range(NO):
                mt = mts[b % 2]
                P.add("sp", _dma(mt[:], mix_d[b * 128:(b + 1) * 128, :]), writes=[mt.b], dma=True)
                for half in range(2):
                    ps = ptr7[half]
                    for j in range(8):
                        kt = half * 8 + j
                        P.add("pe", _tr(ps[:, j, :], mt[:, kt * 128:(kt + 1) * 128], identb[:]), reads=[mt.b, identb.b], writes=[ps.b])
                    dst = mixT[:, half * 8:(half + 1) * 8, b * 128:(b + 1) * 128]
                    P.add("act" if half == 0 else "dve", _act(dst, ps[:], AF.Copy) if half == 0 else _cp(dst, ps[:]),
                          reads=[ps.b], writes=[mixT.sub(b)])
            k = 0
            for g in range(4):
                if g + 1 < 4:
                    load_wo(g + 1)
                wo = wos[g % 2]
                for b in range(NO):
                    ps = pm7[k % 3]
                    k += 1
                    for kt in range(16):
                        P.add("pe", _mm(ps[:], mixT[:, kt, b * 128:(b + 1) * 128], wo[:, kt, :], kt == 0, kt == 15),
                              reads=[mixT.sub(b), wo.b], writes=[ps.b])
                    dst = h1[:, b, g * 512:(g + 1) * 512]
                    P.add("dve", _tt(dst, ps[:], dst, ALU.add), reads=[ps.b, h1.sub(b)], writes=[h1.sub(b)])
            for b in range(NO):
                P.add("sp", _dma(h1_d[b * 128:(b + 1) * 128, :], h1[:, b, :]), reads=[h1.sub(b)], dma=True)
        P.barrier()
        if stop_after == "S7":
            return _finish(nc, P, es, yo)

        with ExitStack() as s8:
            g2b = TT(s8, nc, "g2b", [128, D], F32)
            P.add("sp", _dma(g2b[:], norm2_g.partition_broadcast(128)), writes=[g2b.b], dma=True)
            xnbs = [TT(s8, nc, "xnb8%d" % i, [128, D], BF16) for i in range(2)]
            junk = TT(s8, nc, "junk8", [128, D], BF16)
            ssq = [TT(s8, nc, "ssq8%d" % i, [128, 1], F32) for i in range(2)]
            rsq = [TT(s8, nc, "rsq8%d" % i, [128, 1], F32) for i in range(2)]
            ptr = [TT(s8, nc, "ptr8%d" % i, [128, 8, 128], BF16, psum=True) for i in range(2)]
            for i in range(NO):
                xnb, ss, rs = xnbs[i % 2], ssq[i % 2], rsq[i % 2]
                P.add("act", _act(junk[:], h1[:, i, :], AF.Square, accum_out=ss[:]), reads=[h1.sub(i)], writes=[junk.b, ss.b])
                P.add("dve", _ts(rs[:], ss[:], 1.0 / D, EPS, ALU.mult, ALU.add), reads=[ss.b], writes=[rs.b])
                P.add("act", _act(rs[:], rs[:], AF.Sqrt), reads=[rs.b], writes=[rs.b])
                P.add("dve", _recip(rs[:], rs[:]), reads=[rs.b], writes=[rs.b])
                P.add("dve", _stt(xnb[:], h1[:, i, :], rs[:, 0:1], g2b[:], ALU.mult, ALU.mult),
                      reads=[h1.sub(i), rs.b, g2b.b], writes=[xnb.b])
                for half in range(2):
                    ps = ptr[half]
                    for j in range(8):
                        kt = half * 8 + j
                        P.add("pe", _tr(ps[:, j, :], xnb[:, kt * 128:(kt + 1) * 128], identb[:]), reads=[xnb.b, identb.b], writes=[ps.b])
                    dst = xn2T[:, half * 8:(half + 1) * 8, i * 128:(i + 1) * 128]
                    P.add("act" if half == 0 else "dve", _act(dst, ps[:], AF.Copy) if half == 0 else _cp(dst, ps[:]),
                          reads=[ps.b], writes=[xn2T.sub(i)])
            if "xn2T" in dbg:
                o = dbgout("xn2T", [128, 16 * TO], BF16)
                P.out_dmas.append(P.add("sp", _dma(o, xn2T[:].rearrange("p k t -> p (k t)")), reads=[xn2T.sub(i) for i in range(NO)], dma=True))
        P.barrier()
        st_h1.close()
        if stop_after == "S8":
            return _finish(nc, P, es, yo)

        iota_f = cst[:, C_IOTA:C_IOTA + 128]
        with ExitStack() as s9:
            qT2 = TT(s9, nc, "qT2", [128, 16, TO], BF16)
            skT = TT(s9, nc, "skT", [128, 16, 128], BF16)
            with ExitStack() as s9a:
                wqs = [TT(s9a, nc, "wq%d" % i, [128, 16, 512], BF16) for i in range(2)]
                sk32 = TT(s9a, nc, "sk32", [128, 16, 128], F32)
                skb = TT(s9a, nc, "skb", [128, 16, 128], BF16)
                pm9 = [TT(s9a, nc, "pm9%d" % i, [128, 512], F32, psum=True) for i in range(3)]
                ptr9 = [TT(s9a, nc, "ptr9%d" % i, [128, 8, 128], BF16, psum=True) for i in range(2)]

                def load_wq(g):
                    P.add("pool", _dma(wqs[g % 2][:], wq[:, g * 512:(g + 1) * 512].rearrange("(kt p) c -> p kt c", p=128)),
                          writes=[wqs[g % 2].b], dma=True)

                load_wq(0)
                P.add("sp", _dma(sk32[:], subkeys.rearrange("a k c -> k a c")), writes=[sk32.b], dma=True)
                P.add("dve", _cp(skb[:], sk32[:]), reads=[sk32.b], writes=[skb.b])
                for half in range(2):
                    ps = ptr9[half]
                    for j in range(8):
                        P.add("pe", _tr(ps[:, j, :], skb[:, half * 8 + j, :], identb[:]), reads=[skb.b, identb.b], writes=[ps.b])
                    P.add("act", _act(skT[:, half * 8:(half + 1) * 8, :], ps[:], AF.Copy), reads=[ps.b], writes=[skT.b])
                k = 0
                for g in range(4):
                    if g + 1 < 4:
                        load_wq(g + 1)
                    wt = wqs[g % 2]
                    for sub in range(4):
                        cb = g * 4 + sub
                        for (n0, N) in OWNB:
                            ps = pm9[k % 3]
                            k += 1
                            for kt in range(16):
                                P.add("pe", _mm(ps[:, :N], wt[:, kt, sub * 128:(sub + 1) * 128], xn2T[:, kt, n0:n0 + N], kt == 0, kt == 15),
                                      reads=[wt.b] + [xn2T.sub(i) for i in range(n0 // 128, (n0 + N) // 128)], writes=[ps.b])
                            P.add("act" if k % 2 else "dve",
                                  _act(qT2[:, cb, n0:n0 + N], ps[:, :N], AF.Copy) if k % 2 else _cp(qT2[:, cb, n0:n0 + N], ps[:, :N]),
                                  reads=[ps.b], writes=[qT2.b])
            P.barrier()
            s_sbs = [TT(s9, nc, "s_sb%d" % i, [128, 16, 128], F32) for i in range(1)]
            s2t = TT(s9, nc, "s2t", [128, 128], F32)
            sv = TT(s9, nc, "sv", [128, 16, 16], F32)
            si = TT(s9, nc, "si", [128, 8, 16], U32)
            si_fs = [TT(s9, nc, "si_f%d" % i, [128, 8, 16], F32) for i in range(2)]
            cand = TT(s9, nc, "cand", [128, 16, 16], F32)
            cand2 = TT(s9, nc, "cand2", [128, 16, 16], F32)
            topv = TT(s9, nc, "topv", [128, 16], F32)
            ex16 = TT(s9, nc, "ex16", [128, 16], F32)
            c16 = TT(s9, nc, "c16", [128, 16], F32)
            e2 = TT(s9, nc, "e2", [128, 128], F32)
            thr16s = TT(s9, nc, "thr16s", [128, 8, 16], F32)
            topvs = TT(s9, nc, "topvs", [128, 8, 16], F32)
            rcs = TT(s9, nc, "rcs", [128, 8, 8], F32)
            c16s = TT(s9, nc, "c16s", [128, 8, 16], F32)
            e2s = TT(s9, nc, "e2s", [128, 8, 128], F32)
            iota_b = TT(s9, nc, "iota_b", [128, 128], BF16)
            P.add("dve", _cp(iota_b[:], iota_f), reads=[cst.b], writes=[iota_b.b])
            rc = TT(s9, nc, "rc", [128, 8], F32)
            Rtoks = [TT(s9, nc, "Rtok%d" % i, [128, 2, 16, 128], BF16) for i in range(2)]
            siTs = [TT(s9, nc, "siT%d" % i, [128, 128], BF16) for i in range(2)]
            Rts = [TT(s9, nc, "Rt%d" % i, [128, 32, 128], BF16) for i in range(2)]
            OHs = [TT(s9, nc, "OH%d" % i, [128, 32, 128], BF16) for i in range(2)]
            GTs = TT(s9, nc, "GTs", [128, 128, 128], BF16)
            psc = [TT(s9, nc, "psc%d" % i, [128, 4, 128], F32, psum=True) for i in range(1)]
            pst = TT(s9, nc, "pst", [128, 128], F32, psum=True)
            pgt = [TT(s9, nc, "pgt%d" % i, [128, 128, 8], F32, psum=True) for i in range(1)]
            NCH = NEXP // 128
            OWNB3 = [(TO - NOWN, 344), (TO - NOWN + 344, 344), (TO - NOWN + 688, 344)]
            ucs9 = [TT(s9, nc, "uc9%d" % i, [128, D], BF16) for i in range(3)]
            ucT9 = [TT(s9, nc, "ucT9%d" % i, [128, 16, 128], BF16) for i in range(2)]
            glb = [TT(s9, nc, "glb%d" % i, [128, NOWN], BF16) for i in range(2)]
            pas9 = [TT(s9, nc, "pas9%d" % i, [128, 512], F32, psum=True) for i in range(3)]
            ptu9 = TT(s9, nc, "ptu9", [128, 8, 128], BF16, psum=True)
            ust = {"next": 0}

            def u_load(i):
                P.add("pool", _dma(ucs9[i % 3][:], pu[i * 128:(i + 1) * 128, :]), writes=[ucs9[i % 3].b], dma=True)

            def u_tr(i, half):
                uc, ucT = ucs9[i % 3], ucT9[i % 2]
                for j in range(8):
                    kt = half * 8 + j
                    P.add("pe", _tr(ptu9[:, j, :], uc[:, kt * 128:(kt + 1) * 128], identb[:]), reads=[uc.b, identb.b], writes=[ptu9.b])
                P.add("act", _act(ucT[:, half * 8:(half + 1) * 8, :], ptu9[:], AF.Copy), reads=[ptu9.b], writes=[ucT.sub(half)])

            def u_blk(i, bi):
                n0, N = OWNB3[bi]
                ucT, gl = ucT9[i % 2], glb[i % 2]
                pa = pas9[bi]
                for kt in range(16):
                    P.add("pe", _mm(pa[:, :N], ucT[:, kt, :], xn2T[:, kt, n0:n0 + N], kt == 0, kt == 15),
                          reads=[ucT.sub(kt // 8)] + [xn2T.sub(t) for t in range(n0 // 128, (n0 + N - 1) // 128 + 1)], writes=[pa.b])
                P.add("act", _act(gl[:, bi * 344:(bi + 1) * 344], pa[:, :N], AF.Gelu), reads=[pa.b], writes=[gl.sub(bi)])

            def u_unit():
                i = ust["next"]
                if i >= NCH:
                    return
                ust["next"] = i + 1
                if i + 2 < NCH:
                    u_load(i + 2)
                if i + 1 < NCH:
                    u_tr(i + 1, 0)
                u_blk(i, 0)
                u_blk(i, 1)
                if i + 1 < NCH:
                    u_tr(i + 1, 1)
                u_blk(i, 2)
                P.add("sp", _dma(act_d[i], glb[i % 2][:]), reads=[glb[i % 2].sub(k) for k in range(3)], dma=True)

            def filler(n):
                for _ in range(n):
                    u_unit()

            u_load(0)
            u_load(1)
            u_tr(0, 0)
            u_tr(0, 1)
            R_buf = [Buf("R_d%d" % b) for b in range(NO)]
            def phaseA1(b):
                ts_ = slice(b * 128, (b + 1) * 128)
                s_sb = s_sbs[0]
                for q4 in range(4):
                    for j in range(4):
                        hp = q4 * 4 + j
                        P.add("pe", _mm(psc[0][:, j, :], qT2[:, hp, ts_], skT[:, hp, :], True, True),
                              reads=[qT2.b, skT.b], writes=[psc[0].b])
                    P.add("act", _act(s_sb[:, q4 * 4:(q4 + 1) * 4, :], psc[0][:], AF.Copy), reads=[psc[0].b], writes=[s_sb.sub(q4)])

            def phaseA(b):
                ts_ = slice(b * 128, (b + 1) * 128)
                siT = siTs[b % 2]
                s_sb = s_sbs[0]
                for hp in range(16):
                    h, p = hp // 2, hp % 2
                    sb_ = s_sb.sub(hp // 4)
                    P.add("dve", lambda e, hp=hp: e.max(sv[:, hp, 0:8], s_sb[:, hp, :]), reads=[sb_], writes=[sv.sub(hp)])
                    if p == 0:
                        P.add("dve", lambda e, hp=hp, h=h: e.max_index(si[:, h, 0:8], sv[:, hp, 0:8], s_sb[:, hp, :]),
                              reads=[sb_, sv.sub(hp)], writes=[si.sub(h)])
                    P.add("dve", lambda e, hp=hp: e.match_replace(s2t[:], sv[:, hp, 0:8], s_sb[:, hp, :], NEG),
                          reads=[sb_, sv.sub(hp)], writes=[s2t.b])
                    P.add("dve", lambda e, hp=hp: e.max(sv[:, hp, 8:16], s2t[:]), reads=[s2t.b], writes=[sv.sub(hp)])
                    if p == 0:
                        P.add("dve", lambda e, hp=hp, h=h: e.max_index(si[:, h, 8:16], sv[:, hp, 8:16], s2t[:]),
                              reads=[s2t.b, sv.sub(hp)], writes=[si.sub(h)])
                for h in range(8):
                    hp1, hp2 = 2 * h, 2 * h + 1
                    tv = topvs[:, h, :]
                    P.add("dve", _tt(cand[:], sv[:, hp1, :].unsqueeze(2).to_broadcast([128, 16, 16]),
                                     sv[:, hp2, :].unsqueeze(1).to_broadcast([128, 16, 16]), ALU.add),
                          reads=[sv.sub(hp1), sv.sub(hp2)], writes=[cand.b])
                    P.add("dve", lambda e, h=h: e.max(topvs[:, h, 0:8], cand[:]), reads=[cand.b], writes=[topvs.sub(h)])
                    P.add("dve", lambda e, h=h: e.match_replace(cand2[:], topvs[:, h, 0:8], cand[:], NEG),
                          reads=[cand.b, topvs.sub(h)], writes=[cand2.b])
                    P.add("dve", lambda e, h=h: e.max(topvs[:, h, 8:16], cand2[:]), reads=[cand2.b], writes=[topvs.sub(h)])
                    P.add("dve", _ts(rcs[:, h, 0:1], tv[:, 0:1], -1.0, None, ALU.mult), reads=[topvs.sub(h)], writes=[rcs.sub((h, 0))])
                    P.add("dve", _ts(rcs[:, h, 3:4], sv[:, hp1, 0:1], -1.0, None, ALU.mult), reads=[sv.sub(hp1)], writes=[rcs.sub((h, 3))])
                    P.add("dve", _ts(rcs[:, h, 4:5], sv[:, hp2, 0:1], -1.0, None, ALU.mult), reads=[sv.sub(hp2)], writes=[rcs.sub((h, 4))])
                    P.add("dve", _ts(rcs[:, h, 5:6], tv[:, 15:16], -2e-6, None, ALU.add), reads=[topvs.sub(h)], writes=[rcs.sub((h, 5))])
                    P.add("dve", _ts(thr16s[:, h, :], sv[:, hp1, :], -1.0, rcs[:, h, 5:6], ALU.mult, ALU.add),
                          reads=[sv.sub(hp1), rcs.sub((h, 5))], writes=[thr16s.sub(h)])
                yield
                for h in range(8):
                    hp1, hp2 = 2 * h, 2 * h + 1
                    P.add("act", _act(ex16[:], topvs[:, h, :], AF.Exp, bias=rcs[:, h, 0:1], accum_out=rcs[:, h, 1:2]),
                          reads=[topvs.sub(h), rcs.sub((h, 0))], writes=[ex16.b, rcs.sub((h, 1))])
                    P.add("act", _act(c16s[:, h, :], sv[:, hp1, :], AF.Exp, bias=rcs[:, h, 3:4]),
                          reads=[sv.sub(hp1), rcs.sub((h, 3))], writes=[c16s.sub(h)])
                    P.add("act", _act(e2s[:, h, :], s_sb[:, hp2, :], AF.Exp, bias=rcs[:, h, 4:5]),
                          reads=[s_sb.sub(hp2 // 4), rcs.sub((h, 4))], writes=[e2s.sub(h)])
                for h in range(8):
                    hp1, hp2 = 2 * h, 2 * h + 1
                    Rtok = Rtoks[(h // 2) % 2]
                    hl = h % 2
                    P.add("dve", _recip(rcs[:, h, 2:3], rcs[:, h, 1:2]), reads=[rcs.sub((h, 1))], writes=[rcs.sub((h, 2))])
                    P.add("dve", _ts(c16s[:, h, :], c16s[:, h, :], rcs[:, h, 2:3], None, ALU.mult),
                          reads=[c16s.sub(h), rcs.sub((h, 2))], writes=[c16s.sub(h)])
                    for a_ in range(16):
                        P.add("dve", _stt(Rtok[:, hl, a_, :], s_sb[:, hp2, :], thr16s[:, h, a_:a_ + 1], e2s[:, h, :], ALU.is_ge, ALU.mult),
                              reads=[s_sb.sub(hp2 // 4), thr16s.sub(h), e2s.sub(h)], writes=[Rtok.sub((hl, a_))])
                    P.add("pool", _tt(Rtok[:, hl, :, :], Rtok[:, hl, :, :], c16s[:, h, :].unsqueeze(2).to_broadcast([128, 16, 128]), ALU.mult),
                          reads=[Rtok.sub((hl, a_)) for a_ in range(16)] + [c16s.sub(h)],
                          writes=[Rtok.sub(hl)] + [Rtok.sub((hl, a_)) for a_ in range(16)])
                    if hl == 1:
                        hp_ = h // 2
                        P.add("sp", _dma(R_d[ts_, hp_ * 32:(hp_ + 1) * 32, :], Rtok[:].rearrange("p a b c -> p (a b) c")),
                              reads=[Rtok.sub(0), Rtok.sub(1)] + [Rtok.sub((hl_, a_)) for hl_ in range(2) for a_ in range(16)],
                              writes=[R_buf[b]], dma=True)
                        yield
                P.add("dve", _cp(si_fs[b % 2][:], si[:]), reads=[si.sub(h) for h in range(8)], writes=[si_fs[b % 2].b])

            def phaseB(b):
                ts_ = slice(b * 128, (b + 1) * 128)
                siT = siTs[b % 2]
                si_f = si_fs[b % 2]
                P.add("pe", _tr(pst[:], si_f[:].rearrange("p a b -> p (a b)"), ident_f), reads=[si_f.b, cst.b], writes=[pst.b])
                P.add("act", _act(siT[:], pst[:], AF.Copy), reads=[pst.b], writes=[siT.b])
                def rt_load(hf_):
                    t0_ = b * 128 + hf_ * 32
                    P.add("sp", _dma(Rts[hf_ % 2][:], R_d[t0_:t0_ + 32].rearrange("t p j -> p t j")), reads=[R_buf[b]],
                          writes=[Rts[hf_ % 2].b], dma=True)

                rt_load(0)
                for hf in range(4):
                    Rt, OH = Rts[hf % 2], OHs[hf % 2]
                    if hf + 1 < 4:
                        rt_load(hf + 1)
                    P.add("dve", _tt(OH[:], iota_b[:].unsqueeze(1).to_broadcast([128, 32, 128]),
                                      siT[:, hf * 32:(hf + 1) * 32].unsqueeze(2).to_broadcast([128, 32, 128]), ALU.is_equal),
                          reads=[iota_b.b, siT.b], writes=[OH.b])
                    for t8 in range(4):
                        pg = pgt[0]
                        for tl in range(8):
                            t = t8 * 8 + tl
                            for fh in range(2):
                                P.add("pe", _mm(pg[:, fh * 64:(fh + 1) * 64, tl], Rt[:, t, :], OH[:, t, fh * 64:(fh + 1) * 64], True, True),
                                      reads=[Rt.b, OH.b], writes=[pg.b])
                        tt0 = hf * 32 + t8 * 8
                        P.add("act", _act(GTs[:, :, tt0:tt0 + 8], pg[:], AF.Copy), reads=[pg.b], writes=[GTs.sub(hf)])
                        if (hf * 4 + t8) % 3 != 2:
                            filler(1)
                    if hf < 3:
                        yield
                P.add("sp", _dma(GT_d[:, :, ts_].rearrange("i j t -> j i t"), GTs[:]), reads=[GTs.sub(i) for i in range(4)], dma=True)

            phaseA1(0)
            for b in range(NO):
                gA = phaseA(b)
                gB = phaseB(b - 1) if b >= 1 else iter(())
                next(gB, None)
                next(gB, None)
                next(gA)
                filler(2)
                next(gB, None)
                for _ in gB:
                    pass
                for k_, _ in enumerate(gA):
                    if k_ in (0, 1, 2, 3):
                        filler(1)
                if b + 1 < NO:
                    phaseA1(b + 1)
                if b == 0:
                    filler(8)
            for _ in phaseB(NO - 1):
                pass
            filler(NCH)
        P.barrier()
        if stop_after == "S9":
            return _finish(nc, P, es, yo)

        with ExitStack() as s10:
            yacc = TT(s10, nc, "yacc", [128, NO, D], F32)
            for b in range(NO):
                P.add("sp", _dma(yacc[:, b, :], h1_d[b * 128:(b + 1) * 128, :]), writes=[yacc.sub(b)], dma=True)
            vcs = [TT(s10, nc, "vc%d" % i, [128, GC, D], BF16) for i in range(2)]
            gts = [TT(s10, nc, "gts%d" % i, [128, TO], BF16) for i in range(GC)]
            acs = [TT(s10, nc, "acs%d" % i, [128, NOWN], BF16) for i in range(GC)]
            Wts = [TT(s10, nc, "Wt%d" % i, [128, GC, TO], BF16) for i in range(2)]
            for Wt in Wts:
                P.add("dve", _memset(Wt[:, :, 0:TO - NOWN], 0.0), writes=[Wt.sub("z")])
            pys = [TT(s10, nc, "py%d" % i, [128, 512], F32, psum=True) for i in range(4)]
            cnt10 = {"y": 0}

            def load_a(i):
                P.add("sp", _dma(acs[i % GC][:], act_d[i]), writes=[acs[i % GC].b], dma=True)
                P.add("sp", _dma(gts[i % GC][:], GT_d[i]), writes=[gts[i % GC].b], dma=True)

            def load_v(g, c):
                r0 = (g * GC + c) * 128
                P.add("pool", _dma(vcs[g % 2][:, c, :], pv[r0:r0 + 128, :]), writes=[vcs[g % 2].sub(c)], dma=True)

            def gate(i):
                g, ci = i // GC, i % GC
                Wt = Wts[g % 2]
                P.add("dve" if i % 2 else "pool", _tt(Wt[:, ci, TO - NOWN:TO], acs[i % GC][:], gts[i % GC][:, TO - NOWN:TO], ALU.mult),
                      reads=[acs[i % GC].b, gts[i % GC].b], writes=[Wt.sub(ci)])

            def down_proj(g):
                Wt, vc = Wts[g % 2], vcs[g % 2]
                for b in range(NO):
                    for gq_ in range(4):
                        py = pys[cnt10["y"] % 4]
                        cnt10["y"] += 1
                        for c in range(GC):
                            P.add("pe", _mm(py[:], Wt[:, c, b * 128:(b + 1) * 128], vc[:, c, gq_ * 512:(gq_ + 1) * 512], c == 0, c == GC - 1),
                                  reads=[Wt.sub(c), Wt.sub("z"), vc.sub(c)], writes=[py.b])
                        dst = yacc[:, b, gq_ * 512:(gq_ + 1) * 512]
                        P.add("dve", _tt(dst, py[:], dst, ALU.add), reads=[py.b, yacc.sub(b)], writes=[yacc.sub(b)])

            NG = NCH // GC
            for c in range(GC):
                load_v(0, c)
                load_a(c)
            for c in range(GC):
                gate(c)
            for g in range(NG):
                if g + 1 < NG:
                    for c in range(GC):
                        load_v(g + 1, c)
                        load_a((g + 1) * GC + c)
                    for c in range(GC):
                        gate((g + 1) * GC + c)
                down_proj(g)
            for b in range(NO):
                P.out_dmas.append(P.add("sp", _dma(yo[b * 128:(b + 1) * 128, :], yacc[:, b, :]), reads=[yacc.sub(b)], dma=True))
        return _finish(nc, P, es, yo)


def _finish(nc, P, es, yo):
    P.emit(es)
    return nc


def make_in_maps(inputs, cores=range(8)):
    x = np.asarray(inputs["x"], np.float32)
    meta = np.asarray(inputs["meta_tokens"], np.float32)
    consts = make_consts()
    shared = {
        "consts": consts,
        "norm1_g": inputs["norm1_g"][0], "w_in": inputs["w_in"][0],
        "ml_conv_w": inputs["ml_conv_w"][0], "ml_conv_b": inputs["ml_conv_b"][0],
        "ml_igate_b": inputs["ml_igate_b"][0], "ml_fgate_b": inputs["ml_fgate_b"][0],
        "da_lq1": inputs["da_lq1"][0], "da_lk1": inputs["da_lk1"][0],
        "da_lq2": inputs["da_lq2"][0], "da_lk2": inputs["da_lk2"][0],
        "da_qnorm_g": inputs["da_qnorm_g"][0], "da_knorm_g": inputs["da_knorm_g"][0],
        "da_subln_g": inputs["da_subln_g"][0], "ml_outnorm_g": inputs["ml_outnorm_g"][0],
        "w_out": inputs["w_out"][0], "norm2_g": inputs["norm2_g"][0],
        "peer_wq": inputs["peer_wq"][0],
        "peer_subkeys": inputs["peer_subkeys"][0].reshape(16, 128, 128),
        "peer_u": inputs["peer_u"][0], "peer_v": inputs["peer_v"][0],
    }
    shared = {k: np.ascontiguousarray(np.asarray(v, np.float32)) for k, v in shared.items()}
    maps = []
    for c in cores:
        b, hf = c // 2, c % 2
        ntok = 1032 * (hf + 1)
        h = np.concatenate([meta, x[b]], axis=0)[:ntok]
        xc = np.zeros((TC, D), np.float32)
        xc[TC - ntok:] = h
        nm = np.full((TC,), NEG, np.float32)
        nm[TC - ntok:] = 0.0
        va = np.zeros((TC,), np.float32)
        va[TC - ntok:] = 1.0
        m = dict(shared)
        m.update({"xc": xc, "nmask": nm, "valid": va})
        maps.append(m)
    return maps


_NC_CACHE = {}


def kernel(**inputs):
    if "nc" not in _NC_CACHE:
        _NC_CACHE["nc"] = build()
    nc = _NC_CACHE["nc"]
    maps = make_in_maps(inputs)
    res = run_bass_kernel_spmd(nc, maps, core_ids=list(range(8)))
    out = np.zeros((4, 2048, 2048), np.float32)
    for c in range(8):
        b, hf = c // 2, c % 2
        y = np.asarray(res.results[c]["yo"], np.float32)[TO - NOWN:]
        if hf == 0:
            out[b, 0:1016] = y[16:]
        else:
            out[b, 1016:2048] = y
    return out
```

```python
import numpy as np
import ml_dtypes
from contextlib import ExitStack
import concourse.bass as bass
import concourse.mybir as mybir
from concourse.bass_utils import run_bass_kernel_spmd

F32 = mybir.dt.float32
BF16 = mybir.dt.bfloat16
U32 = mybir.dt.uint32
I32 = mybir.dt.int32
AF = mybir.ActivationFunctionType
ALU = mybir.AluOpType
AX = mybir.AxisListType

ENGS = ("pe", "act", "dve", "pool", "sp")
DMA_ENGS = ("sp", "pool")
SEM_CAP = 6000
NDMA_SEM = 6


class Buf:
    __slots__ = ("name", "w", "r")

    def __init__(self, name=""):
        self.name = name
        self.w = None
        self.r = []


class Op:
    __slots__ = ("eng", "fn", "deps", "id", "is_dma", "need", "sem", "val", "dsem")

    def __init__(self, eng, fn, is_dma):
        self.eng = eng
        self.fn = fn
        self.is_dma = is_dma
        self.deps = []
        self.need = False
        self.sem = None
        self.val = None
        self.dsem = None


class Prog:
    def __init__(self, nc):
        self.nc = nc
        self.ops = []
        self.per_eng = {e: [] for e in ENGS}
        self.dma_rr = {e: 0 for e in DMA_ENGS}
        self.dma_last = {e: [None] * NDMA_SEM for e in DMA_ENGS}
        self.bar = {e: None for e in ENGS}
        self.out_dmas = []
        self.bar_from = 0

    def add(self, eng, fn, reads=(), writes=(), dma=False, extra_deps=()):
        op = Op(eng, fn, dma)
        op.id = len(self.ops)
        deps = set(extra_deps)
        for b in reads:
            if b.w is not None:
                deps.add(b.w)
        for b in writes:
            if b.w is not None:
                deps.add(b.w)
            deps.update(b.r)
        if self.bar[eng] is not None:
            deps.add(self.bar[eng])
            self.bar[eng] = None
        if dma:
            k = self.dma_rr[eng]
            self.dma_rr[eng] = (k + 1) % NDMA_SEM
            op.dsem = k
            prev = self.dma_last[eng][k]
            if prev is not None:
                deps.add(prev)
            self.dma_last[eng][k] = op.id
        deps.discard(op.id)
        op.deps = sorted(deps)
        for b in reads:
            if not dma:
                b.r = [i for i in b.r if not (self.ops[i].eng == eng and not self.ops[i].is_dma)]
            b.r.append(op.id)
        for b in writes:
            b.w = op.id
            b.r = []
        self.ops.append(op)
        self.per_eng[eng].append(op)
        return op.id

    def barrier(self):
        deps = []
        for e in ENGS:
            for o in reversed(self.per_eng[e]):
                if not o.is_dma:
                    deps.append(o.id)
                    break
        deps += [o.id for o in self.ops[self.bar_from:] if o.is_dma]
        bid = self.add("act", lambda e: e.nop(), extra_deps=deps)
        self.bar_from = len(self.ops)
        for e in ENGS:
            self.bar[e] = bid
        self.bar["act"] = None
        return bid

    def emit(self, es):
        nc = self.nc
        ops = self.ops
        for o in ops:
            for d in o.deps:
                ops[d].need = True
        for i in self.out_dmas:
            ops[i].need = True
        sems = {}

        def getsem(key):
            if key not in sems:
                sems[key] = es.enter_context(nc.semaphore("s_%s_%s_%s" % key))
            return sems[key]

        cnt = {e: 0 for e in ENGS}
        dcnt = {(e, k): 0 for e in DMA_ENGS for k in range(NDMA_SEM)}
        for o in ops:
            if o.is_dma:
                c = dcnt[(o.eng, o.dsem)]
                gen = c // (SEM_CAP // 16)
                o.sem = getsem(("d", o.eng, "%d_%d" % (o.dsem, gen)))
                o.val = 16 * (c % (SEM_CAP // 16) + 1)
                dcnt[(o.eng, o.dsem)] = c + 1
            elif o.need:
                c = cnt[o.eng]
                gen = c // SEM_CAP
                o.sem = getsem(("c", o.eng, gen))
                o.val = c % SEM_CAP + 1
                cnt[o.eng] = c + 1
        engmap = {"pe": "tensor", "act": "scalar", "dve": "vector", "pool": "gpsimd", "sp": "sync"}
        blk = es.enter_context(nc.Block())
        for en in ENGS:
            lst = self.per_eng[en]
            final_waits = [ops[i] for i in self.out_dmas] if en == "sp" else []

            def body(e, lst=lst, en=en, final_waits=final_waits):
                waited = {}
                for o in lst:
                    for d in o.deps:
                        s = ops[d]
                        if s.eng == "pe" and en == "pe" and not s.is_dma:
                            continue
                        key = id(s.sem)
                        if waited.get(key, 0) >= s.val and not s.is_dma:
                            continue
                        if s.is_dma and waited.get(key, 0) >= s.val:
                            continue
                        e.wait_ge(s.sem, s.val)
                        waited[key] = max(waited.get(key, 0), s.val)
                    ins = o.fn(e)
                    if o.is_dma:
                        ins.then_inc(o.sem, 16)
                    elif o.need:
                        ins.then_inc(o.sem, 1)
                for s in final_waits:
                    e.wait_ge(s.sem, s.val)

            getattr(blk, engmap[en])(body)


D = 2048
NT = 17
TC = NT * 128
NO = 9
TO = NO * 128
OWN0 = TC - TO
NOWN = 1032
EPS = 1e-6
NEG = -1.0e30
LAMBDA_INIT = 0.2
DIN = 6152
NEXP = 16384
GC = 4

C_ID = 0
C_IOTA = 128
C_MADD = 256
C_TRI = 384
C_ONES = 512
C_SEL = 640
C_AB = 1152
C_PIDX = 1220
NCONST = 1221


def make_consts():
    c = np.zeros((128, NCONST), np.float32)
    p = np.arange(128)[:, None].astype(np.float32)
    t = np.arange(128)[None, :].astype(np.float32)
    c[:, C_ID:C_ID + 128] = (p == t)
    c[:, C_IOTA:C_IOTA + 128] = t
    c[:, C_MADD:C_MADD + 128] = np.where(p <= t, 0.0, 30000.0)
    c[:, C_TRI:C_TRI + 128] = (t >= p)
    c[:, C_ONES:C_ONES + 128] = 1.0
    for h in range(4):
        c[h, C_SEL + h * 128:C_SEL + (h + 1) * 128] = 1.0
    start = 2.0 ** (-8.0 / 4)
    for h in range(4):
        slope = np.float32(start ** (h + 1))
        for d in range(NT):
            c[:, C_AB + h * NT + d] = slope * (p[:, 0] - 128.0 * d)
    c[:, C_PIDX] = p[:, 0]
    return c


class TT:
    def __init__(self, es, nc, name, shape, dtype, psum=False):
        cm = nc.psum_tensor(name, shape, dtype) if psum else nc.sbuf_tensor(name, shape, dtype)
        self.t = es.enter_context(cm)
        self.b = Buf(name)
        self.subs = {}

    def __getitem__(self, k):
        return self.t[k]

    def sub(self, key):
        if key not in self.subs:
            self.subs[key] = Buf("%s/%s" % (self.b.name, key))
        return self.subs[key]


def _mm(out, lhsT, rhs, start, stop):
    return lambda e: e.matmul(out, lhsT, rhs, start=start, stop=stop)


def _tr(out, in_, ident):
    return lambda e: e.transpose(out, in_, ident)


def _act(out, in_, func, bias=None, scale=None, accum_out=None):
    kw = {}
    if bias is not None:
        kw["bias"] = bias
    if scale is not None:
        kw["scale"] = scale
    if accum_out is not None:
        kw["accum_out"] = accum_out
    return lambda e: e.activation(out, in_, func, **kw)


def _cp(out, in_):
    return lambda e: e.tensor_copy(out, in_)


def _tt(out, in0, in1, op):
    return lambda e: e.tensor_tensor(out, in0, in1, op)


def _ts(out, in0, s1, s2, op0, op1=None):
    if op1 is None:
        return lambda e: e.tensor_scalar(out, in0, s1, None, op0)
    return lambda e: e.tensor_scalar(out, in0, s1, s2, op0, op1)


def _stt(out, in0, scalar, in1, op0, op1):
    return lambda e: e.scalar_tensor_tensor(out, in0, scalar, in1, op0, op1)


def _dma(out, in_):
    return lambda e: e.dma_start(out=out, in_=in_)


def _recip(out, in_):
    return lambda e: e.reciprocal(out, in_)


def _memset(ap, v):
    return lambda e: e.memset(ap, v)


CTX_BLKS = [(0, 512), (512, 512), (1024, 512), (1536, 512), (2048, 128)]
OWN_BLKS = [(1024, 512), (1536, 512), (2048, 128)]


def build(stop_after=None, dbg=()):
    nc = bass.Bass("TRN2", target_bir_lowering=False)
    P = Prog(nc)
    dbg = set(dbg)

    def din(name, shape, dt=F32):
        return nc.dram_tensor(name, shape, dt, kind="ExternalInput").ap()

    def dscr(name, shape, dt):
        kind = "ExternalOutput" if name in dbg else "Internal"
        return nc.dram_tensor(name, shape, dt, kind=kind).ap()

    xc = din("xc", [TC, D])
    nmask = din("nmask", [TC])
    valid = din("valid", [TC])
    consts = din("consts", [128, NCONST])
    norm1_g = din("norm1_g", [D])
    w_in = din("w_in", [D, DIN])
    conv_w = din("ml_conv_w", [4, 1024])
    conv_b = din("ml_conv_b", [1024])
    igate_b = din("ml_igate_b", [4])
    fgate_b = din("ml_fgate_b", [4])
    lq1 = din("da_lq1", [128])
    lk1 = din("da_lk1", [128])
    lq2 = din("da_lq2", [128])
    lk2 = din("da_lk2", [128])
    qng = din("da_qnorm_g", [128])
    kng = din("da_knorm_g", [128])
    subln_g = din("da_subln_g", [256])
    outnorm_g = din("ml_outnorm_g", [4, 256])
    w_out = din("w_out", [D, D])
    norm2_g = din("norm2_g", [D])
    wq = din("peer_wq", [D, D])
    subkeys = din("peer_subkeys", [16, 128, 128])
    pu = din("peer_u", [NEXP, D])
    pv = din("peer_v", [NEXP, D])
    yo = nc.dram_tensor("yo", [TO, D], F32, kind="ExternalOutput").ap()

    qT_d = dscr("qT_d", [8, 128, TO], BF16)
    kT_d = dscr("kT_d", [8, 128, TC], BF16)
    mlqk_d = dscr("mlqk_d", [8, 128, TC], F32)
    v_d = dscr("v_d", [TC, 1024], BF16)
    mlv_d = dscr("mlv_d", [TC, 1024], BF16)
    mix_d = dscr("mix_d", [TO, D], BF16)
    sigo_d = dscr("sigo_d", [TO, 1024], BF16)
    h1_d = dscr("h1_d", [TO, D], F32)
    R_d = dscr("R_d", [TO, 128, 128], BF16)
    GT_d = dscr("GT_d", [128, 128, TO], BF16)
    act_d = dscr("act_d", [NEXP // 128, 128, NOWN], BF16)
    dbg_d = {}

    def dbgout(name, shape, dt=F32):
        dbg_d[name] = nc.dram_tensor("dbg_" + name, shape, dt, kind="ExternalOutput").ap()
        return dbg_d[name]

    es = ExitStack()
    with es:
        ncd = es.enter_context(nc.allow_non_contiguous_dma(reason="small strided param loads"))
        cst = TT(es, nc, "cst", [128, NCONST], F32)
        identb = TT(es, nc, "identb", [128, 128], BF16)
        onesb = TT(es, nc, "onesb", [128, 128], BF16)
        P.add("sp", _dma(cst[:], consts), writes=[cst.b], dma=True)
        P.add("dve", _cp(identb[:], cst[:, C_ID:C_ID + 128]), reads=[cst.b], writes=[identb.b])
        P.add("dve", _cp(onesb[:], cst[:, C_ONES:C_ONES + 128]), reads=[cst.b], writes=[onesb.b])
        ident_f = cst[:, C_ID:C_ID + 128]

        DK_S = 128.0 ** -0.5
        gq = TT(es, nc, "gq", [128, 1], F32)
        gk = TT(es, nc, "gk", [128, 1], F32)
        P.add("sp", _dma(gq[:], qng.rearrange("(p o) -> p o", o=1)), writes=[gq.b], dma=True)
        P.add("sp", _dma(gk[:], kng.rearrange("(p o) -> p o", o=1)), writes=[gk.b], dma=True)
        P.add("dve", _ts(gq[:], gq[:], DK_S, None, ALU.mult), reads=[gq.b], writes=[gq.b])
        st_b = es.enter_context(ExitStack())
        urep = [TT(st_b, nc, "urep%d" % h, [128, TC], F32) for h in range(4)]
        acen = TT(st_b, nc, "acen", [128, NT, 8], F32)
        st_a = es.enter_context(ExitStack())
        xnT = TT(st_a, nc, "xnT", [128, 16, TC], BF16)

        s1 = es.enter_context(ExitStack())
        if True:
            g1b = TT(s1, nc, "g1b", [128, D], F32)
            P.add("sp", _dma(g1b[:], norm1_g.partition_broadcast(128)), writes=[g1b.b], dma=True)
            xts = [TT(s1, nc, "xt%d" % i, [128, D], F32) for i in range(2)]
            xnbs = [TT(s1, nc, "xnb%d" % i, [128, D], BF16) for i in range(2)]
            junk = TT(s1, nc, "junk", [128, D], BF16)
            ssq = [TT(s1, nc, "ssq%d" % i, [128, 1], F32) for i in range(2)]
            rsq = [TT(s1, nc, "rsq%d" % i, [128, 1], F32) for i in range(2)]
            ptr = [TT(s1, nc, "ptr%d" % i, [128, 8, 128], BF16, psum=True) for i in range(2)]
            def s1_tile(i):
                xt, xnb, ss, rs = xts[i % 2], xnbs[i % 2], ssq[i % 2], rsq[i % 2]
                P.add("sp", _dma(xt[:], xc[i * 128:(i + 1) * 128, :]), writes=[xt.b], dma=True)
                P.add("act", _act(junk[:], xt[:], AF.Square, accum_out=ss[:]), reads=[xt.b], writes=[junk.b, ss.b])
                P.add("dve", _ts(rs[:], ss[:], 1.0 / D, EPS, ALU.mult, ALU.add), reads=[ss.b], writes=[rs.b])
                P.add("act", _act(rs[:], rs[:], AF.Sqrt), reads=[rs.b], writes=[rs.b])
                P.add("dve", _recip(rs[:], rs[:]), reads=[rs.b], writes=[rs.b])
                P.add("dve", _stt(xnb[:], xt[:], rs[:, 0:1], g1b[:], ALU.mult, ALU.mult),
                      reads=[xt.b, rs.b, g1b.b], writes=[xnb.b])
                for half in range(2):
                    ps = ptr[half]
                    for j in range(8):
                        kt = half * 8 + j
                        P.add("pe", _tr(ps[:, j, :], xnb[:, kt * 128:(kt + 1) * 128], identb[:]),
                              reads=[xnb.b, identb.b], writes=[ps.b])
                    P.add("act" if half == 0 else "dve",
                          (_act(xnT[:, half * 8:(half + 1) * 8, i * 128:(i + 1) * 128], ps[:], AF.Copy) if half == 0
                           else _cp(xnT[:, half * 8:(half + 1) * 8, i * 128:(i + 1) * 128], ps[:])),
                          reads=[ps.b], writes=[xnT.sub(i)])
            for i in range(NT):
                s1_tile(i)
        if "xnT" in dbg:
            o = dbgout("xnT", [128, 16 * TC], BF16)
            P.out_dmas.append(P.add("sp", _dma(o, xnT[:].rearrange("p k t -> p (k t)")), reads=[xnT.sub(i) for i in range(NT)], dma=True))
        if stop_after == "S1":
            P.barrier()
            return _finish(nc, P, es, yo)

        with ExitStack() as s2:
            wts = [TT(s2, nc, "wt%d" % i, [128, 16, 512], BF16) for i in range(2)]
            pmm = [TT(s2, nc, "pmm%d" % i, [128, 512], F32, psum=True) for i in range(4)]
            pss = [TT(s2, nc, "pss%d" % i, [128, 512], F32, psum=True) for i in range(2)]
            sqs = [TT(s2, nc, "sq%d" % i, [128, 512], BF16) for i in range(2)]
            rss = [TT(s2, nc, "rs%d" % i, [128, 512], F32) for i in range(2)]
            qns = [TT(s2, nc, "qn%d" % i, [128, 512], BF16) for i in range(2)]
            rws = [TT(s2, nc, "rw%d" % i, [128, 512], F32) for i in range(2)]
            vts = [TT(s2, nc, "vt%d" % i, [128, 512], BF16) for i in range(2)]
            cnt = {"mm": 0, "e": 0}

            def xn_bufs(n0, N):
                return [xnT.sub(i) for i in range(n0 // 128, (n0 + N) // 128)]

            def load_w(it_):
                wt = wts[it_ % 2]
                gc_ = order[it_]
                src = w_in[:, gc_ * 512:(gc_ + 1) * 512].rearrange("(kt p) c -> p kt c", p=128)
                P.add("pool", _dma(wt[:], src), writes=[wt.b], dma=True)

            groups = ([("qn", qT_d, True, gq)] * 2 + [("qn", kT_d, False, gk)] * 2 +
                      [("tm", v_d, False, AF.Copy)] * 2 + [("raw", mlqk_d, False, None)] * 2 +
                      [("tm", mlv_d, False, AF.Copy)] * 2 + [("tm", sigo_d, True, AF.Sigmoid)] * 2)
            order = [4, 5, 8, 9, 2, 3, 6, 7, 0, 1, 10, 11]
            load_w(0)
            for it_, g in enumerate(order):
                kind, dst, own, extra = groups[g]
                if it_ + 1 < len(order):
                    load_w(it_ + 1)
                wt = wts[it_ % 2]
                gl = g % 2
                if kind in ("qn", "raw"):
                    blks = OWN_BLKS if own else CTX_BLKS
                    for sub in range(4):
                        cb = gl * 4 + sub
                        for (n0, N) in blks:
                            ps = pmm[cnt["mm"] % 4]
                            cnt["mm"] += 1
                            for kt in range(16):
                                P.add("pe", _mm(ps[:, :N], wt[:, kt, sub * 128:(sub + 1) * 128], xnT[:, kt, n0:n0 + N],
                                                kt == 0, kt == 15), reads=[wt.b] + xn_bufs(n0, N), writes=[ps.b])
                            d0 = n0 - (OWN0 if own else 0)
                            k = cnt["e"] % 2
                            cnt["e"] += 1
                            if kind == "raw":
                                rw = rws[k]
                                P.add("act", _act(rw[:, :N], ps[:, :N], AF.Copy), reads=[ps.b], writes=[rw.b])
                                P.add("sp", _dma(dst[cb, :, d0:d0 + N], rw[:, :N]), reads=[rw.b], dma=True)
                            else:
                                sq, rs, qn, pz = sqs[k], rss[k], qns[k], pss[k]
                                P.add("act", _act(sq[:, :N], ps[:, :N], AF.Square), reads=[ps.b], writes=[sq.b])
                                P.add("pe", _mm(pz[:, :N], onesb[:], sq[:, :N], True, True), reads=[onesb.b, sq.b], writes=[pz.b])
                                P.add("dve", _ts(rs[:, :N], pz[:, :N], 1.0 / 128, EPS, ALU.mult, ALU.add), reads=[pz.b], writes=[rs.b])
                                P.add("act", _act(rs[:, :N], rs[:, :N], AF.Ln), reads=[rs.b], writes=[rs.b])
                                P.add("act", _act(rs[:, :N], rs[:, :N], AF.Exp, scale=-0.5), reads=[rs.b], writes=[rs.b])
                                P.add("dve", _stt(qn[:, :N], ps[:, :N], extra[:, 0:1], rs[:, :N], ALU.mult, ALU.mult),
                                      reads=[ps.b, extra.b, rs.b], writes=[qn.b])
                                P.add("sp", _dma(dst[cb, :, d0:d0 + N], qn[:, :N]), reads=[qn.b], dma=True)
                else:
                    tiles = range(8, NT) if own else range(NT)
                    for i in tiles:
                        ps = pmm[cnt["mm"] % 4]
                        cnt["mm"] += 1
                        for kt in range(16):
                            P.add("pe", _mm(ps[:], xnT[:, kt, i * 128:(i + 1) * 128], wt[:, kt, :], kt == 0, kt == 15),
                                  reads=[wt.b, xnT.sub(i)], writes=[ps.b])
                        k = cnt["e"] % 2
                        cnt["e"] += 1
                        vt = vts[k]
                        P.add("act", _act(vt[:], ps[:], extra), reads=[ps.b], writes=[vt.b])
                        r0 = (i - 8 if own else i) * 128
                        P.add("sp", _dma(dst[r0:r0 + 128, gl * 512:(gl + 1) * 512], vt[:]), reads=[vt.b], dma=True)
        P.barrier()
        s1.close()
        if stop_after == "S2":
            return _finish(nc, P, es, yo)

        with ExitStack() as s3:
            wg = TT(s3, nc, "wg", [128, 16, 8], BF16)
            P.add("pool", _dma(wg[:], w_in[:, 6144:6152].rearrange("(kt p) c -> p kt c", p=128)), writes=[wg.b], dma=True)
            nm4 = TT(s3, nc, "nm4", [4, TC], F32)
            va4 = TT(s3, nc, "va4", [4, TC], F32)
            P.add("sp", _dma(nm4[:], nmask.partition_broadcast(4)), writes=[nm4.b], dma=True)
            P.add("sp", _dma(va4[:], valid.partition_broadcast(4)), writes=[va4.b], dma=True)
            ib = TT(s3, nc, "ib", [4, 1], F32)
            fbn = TT(s3, nc, "fbn", [4, 1], F32)
            P.add("sp", _dma(ib[:], igate_b.rearrange("(p o) -> p o", o=1)), writes=[ib.b], dma=True)
            P.add("sp", _dma(fbn[:], fgate_b.rearrange("(p o) -> p o", o=1)), writes=[fbn.b], dma=True)
            P.add("dve", _ts(fbn[:], fbn[:], -1.0, None, ALU.mult), reads=[fbn.b], writes=[fbn.b])
            ones4 = TT(s3, nc, "ones4", [4, TC], F32)
            P.add("dve", _memset(ones4[:], 1.0), writes=[ones4.b])
            g_li = TT(s3, nc, "g_li", [4, TC], F32)
            g_lf = TT(s3, nc, "g_lf", [4, TC], F32)
            g_B = TT(s3, nc, "g_B", [4, TC], F32)
            g_A = TT(s3, nc, "g_A", [4, TC], F32)
            g_u = TT(s3, nc, "g_u", [4, TC], F32)
            g_en = TT(s3, nc, "g_en", [4, TC], F32)
            g_t = [TT(s3, nc, "g_t%d" % i, [4, 512], F32) for i in range(2)]
            pli = TT(s3, nc, "pli", [128, 512], F32, psum=True)
            plf = TT(s3, nc, "plf", [128, 512], F32, psum=True)
            pur = [TT(s3, nc, "pur%d" % i, [128, 512], F32, psum=True) for i in range(2)]
            pac = TT(s3, nc, "pac", [128, NT, 8], F32, psum=True)
            for bi, (n0, N) in enumerate(CTX_BLKS):
                xb = [xnT.sub(i) for i in range(n0 // 128, (n0 + N) // 128)]
                for kt in range(16):
                    P.add("pe", _mm(pli[0:4, :N], wg[:, kt, 0:4], xnT[:, kt, n0:n0 + N], kt == 0, kt == 15),
                          reads=[wg.b] + xb, writes=[pli.b])
                for kt in range(16):
                    P.add("pe", _mm(plf[0:4, :N], wg[:, kt, 4:8], xnT[:, kt, n0:n0 + N], kt == 0, kt == 15),
                          reads=[wg.b] + xb, writes=[plf.b])
                P.add("dve", _stt(g_li[:, n0:n0 + N], pli[0:4, :N], ib[:, 0:1], nm4[:, n0:n0 + N], ALU.add, ALU.add),
                      reads=[pli.b, ib.b, nm4.b], writes=[g_li.sub(bi)])
                t0, t1 = g_t
                P.add("act", _act(t0[:, :N], plf[0:4, :N], AF.Exp, bias=fbn[:, 0:1], scale=-1.0),
                      reads=[plf.b, fbn.b], writes=[t0.b])
                P.add("act", _act(t1[:, :N], t0[:, :N], AF.Ln, bias=1.0), reads=[t0.b], writes=[t1.b])
                P.add("dve", _stt(g_lf[:, n0:n0 + N], t1[:, :N], -1.0, va4[:, n0:n0 + N], ALU.mult, ALU.mult),
                      reads=[t1.b, va4.b], writes=[g_lf.sub(bi)])
            lib = [g_li.sub(i) for i in range(5)]
            lfb = [g_lf.sub(i) for i in range(5)]
            if stop_after == "S3a":
                P.barrier()
                return _finish(nc, P, es, yo)
            P.add("dve", lambda e: e.tensor_tensor_scan(g_B[:], ones4[:], g_lf[:], 0.0, ALU.mult, ALU.add),
                  reads=[ones4.b] + lfb, writes=[g_B.b])
            P.add("dve", _tt(g_A[:], g_li[:], g_B[:], ALU.subtract), reads=lib + [g_B.b], writes=[g_A.b])
            P.add("dve", lambda e: e.tensor_tensor_scan(g_u[:], ones4[:], g_A[:], 0.0, ALU.mult, ALU.max),
                  reads=[ones4.b, g_A.b], writes=[g_u.b])
            P.add("dve", _tt(g_en[:], g_u[:], g_B[:], ALU.add), reads=[g_u.b, g_B.b], writes=[g_en.b])
            P.add("act", _act(g_en[:], g_en[:], AF.Exp, scale=-2.0), reads=[g_en.b], writes=[g_en.b])
            if stop_after == "S3b":
                P.barrier()
                return _finish(nc, P, es, yo)
            k = 0
            for h in range(4):
                for (n0, N) in CTX_BLKS:
                    ps = pur[k % 2]
                    k += 1
                    P.add("pe", _mm(ps[:, :N], cst[0:4, C_SEL + h * 128:C_SEL + (h + 1) * 128], g_u[:, n0:n0 + N], True, True),
                          reads=[cst.b, g_u.b], writes=[ps.b])
                    P.add("act", _act(urep[h][:, n0:n0 + N], ps[:, :N], AF.Copy), reads=[ps.b], writes=[urep[h].b])
            if stop_after == "S3c":
                P.barrier()
                return _finish(nc, P, es, yo)
            for i in range(NT):
                P.add("pe", _mm(pac[:, i, 0:4], g_A[:, i * 128:(i + 1) * 128], cst[0:4, C_ID:C_ID + 4], True, True),
                      reads=[g_A.b, cst.b], writes=[pac.b])
                P.add("pe", _mm(pac[:, i, 4:8], g_en[:, i * 128:(i + 1) * 128], cst[0:4, C_ID:C_ID + 4], True, True),
                      reads=[g_en.b, cst.b], writes=[pac.b])
            P.add("act", _act(acen[:], pac[:], AF.Copy), reads=[pac.b], writes=[acen.b])
            if "gates" in dbg:
                o = dbgout("gates", [4, 4 * TC])
                for j, gt_ in enumerate((g_li, g_lf, g_u, g_en)):
                    P.out_dmas.append(P.add("sp", _dma(o[:, j * TC:(j + 1) * TC], gt_[:]),
                                            reads=[gt_.b] + [gt_.sub(i) for i in range(5)], dma=True))
        P.barrier()
        st_a.close()
        if stop_after == "S3":
            return _finish(nc, P, es, yo)
        st_c = es.enter_context(ExitStack())
        mqT = [TT(st_c, nc, "mqT%d" % h, [128, TC], BF16) for h in range(4)]
        mkT = [TT(st_c, nc, "mkT%d" % h, [128, TC], BF16) for h in range(4)]
        mk_tok = TT(st_c, nc, "mk_tok", [128, NT, 4, 128], BF16)

        with ExitStack() as s4:
            cw = TT(s4, nc, "cw", [128, 4, 8], F32)
            cbias = TT(s4, nc, "cbias", [128, 8], F32)
            for j in range(4):
                P.add("sp", _dma(cw[:, j, :], conv_w[j].rearrange("(cb p) -> p cb", p=128)), writes=[cw.sub(j)], dma=True)
            P.add("sp", _dma(cbias[:], conv_b.rearrange("(cb p) -> p cb", p=128)), writes=[cbias.b], dma=True)
            xcs = [TT(s4, nc, "xcv%d" % i, [128, 3 + TC], F32) for i in range(2)]
            accs = [TT(s4, nc, "acc%d" % i, [128, TC], F32) for i in range(2)]
            ptk = [TT(s4, nc, "ptk%d" % i, [128, 4, 128], BF16, psum=True) for i in range(2)]
            for xv in xcs:
                P.add("dve", _memset(xv[:, 0:3], 0.0), writes=[xv.sub("pad")])
            for cb in range(8):
                xv, acc = xcs[cb % 2], accs[cb % 2]
                P.add("sp", _dma(xv[:, 3:3 + TC], mlqk_d[cb]), writes=[xv.b], dma=True)
                rd = [xv.b, xv.sub("pad"), cbias.b] + [cw.sub(j) for j in range(4)]
                P.add("dve", _ts(acc[:], xv[:, 0:TC], cw[:, 0, cb:cb + 1], cbias[:, cb:cb + 1], ALU.mult, ALU.add),
                      reads=rd, writes=[acc.b])
                for j in range(1, 4):
                    P.add("dve", _stt(acc[:], xv[:, j:j + TC], cw[:, j, cb:cb + 1], acc[:], ALU.mult, ALU.add),
                          reads=rd + [acc.b], writes=[acc.b])
                if cb < 4:
                    P.add("act", _act(mqT[cb][:], acc[:], AF.Silu), reads=[acc.b], writes=[mqT[cb].b])
                else:
                    P.add("act", _act(acc[:], acc[:], AF.Silu), reads=[acc.b], writes=[acc.b])
                    P.add("dve", _ts(mkT[cb - 4][:], acc[:], DK_S, None, ALU.mult), reads=[acc.b], writes=[mkT[cb - 4].b])
            for i in range(NT):
                ps = ptk[i % 2]
                for h in range(4):
                    P.add("pe", _tr(ps[:, h, :], mkT[h][:, i * 128:(i + 1) * 128], identb[:]),
                          reads=[mkT[h].b, identb.b], writes=[ps.b])
                P.add("act" if i % 2 == 0 else "dve",
                      _act(mk_tok[:, i, :, :], ps[:], AF.Copy) if i % 2 == 0 else _cp(mk_tok[:, i, :, :], ps[:]),
                      reads=[ps.b], writes=[mk_tok.b])
            if "s4" in dbg:
                o = dbgout("mqk", [128, 8 * TC], BF16)
                for h in range(4):
                    P.out_dmas.append(P.add("sp", _dma(o[:, h * TC:(h + 1) * TC], mqT[h][:]), reads=[mqT[h].b], dma=True))
                    P.out_dmas.append(P.add("sp", _dma(o[:, (4 + h) * TC:(5 + h) * TC], mkT[h][:]), reads=[mkT[h].b], dma=True))
                o2 = dbgout("mktok", [128, NT * 4 * 128], BF16)
                P.out_dmas.append(P.add("sp", _dma(o2, mk_tok[:].rearrange("p a b c -> p (a b c)")), reads=[mk_tok.b], dma=True))
                o3 = dbgout("urep", [4, TC])
                for h in range(4):
                    P.out_dmas.append(P.add("sp", _dma(o3[h:h + 1, :], urep[h][5:6, :]), reads=[urep[h].b], dma=True))
                o4 = dbgout("acen", [128, NT * 8])
                P.out_dmas.append(P.add("sp", _dma(o4, acen[:].rearrange("p a b -> p (a b)")), reads=[acen.b], dma=True))
        P.barrier()
        if stop_after == "S4":
            return _finish(nc, P, es, yo)

        madd = cst[:, C_MADD:C_MADD + 128]

        def rr(gens):
            gens = list(gens)
            while gens:
                for g_ in list(gens):
                    try:
                        next(g_)
                    except StopIteration:
                        gens.remove(g_)

        with ExitStack() as s5:
            zcol = TT(s5, nc, "zcol", [128, 1], F32)
            P.add("dve", _memset(zcol[:], 0.0), writes=[zcol.b])
            vtl = [TT(s5, nc, "vtl%d" % i, [128, 4, 257], BF16) for i in range(2)]
            for vt in vtl:
                P.add("dve", _memset(vt[:, :, 256:257], 1.0), writes=[vt.sub("ones")])
            Cf = [TT(s5, nc, "Cf%d" % h, [128, 257], F32) for h in range(4)]
            Cb = [TT(s5, nc, "Cb%d" % h, [128, 257], BF16) for h in range(4)]
            for h in range(4):
                P.add("dve", _memset(Cf[h][:], 0.0), writes=[Cf[h].b])
                P.add("dve", _memset(Cb[h][:], 0.0), writes=[Cb[h].b])
            gout = TT(s5, nc, "gout", [128, 4, 256], F32)
            P.add("sp", _dma(gout[:].rearrange("p a b -> p (a b)"), outnorm_g.rearrange("a b -> (a b)").partition_broadcast(128)),
                  writes=[gout.b], dma=True)
            sigs = [TT(s5, nc, "sig%d" % i, [128, 1024], BF16) for i in range(2)]
            mouts = [TT(s5, nc, "mout%d" % i, [128, 1024], BF16) for i in range(2)]
            R4 = range(4)
            dms = [TT(s5, nc, "dm%d" % i, [128, 128], F32) for i in R4]
            wts_ = [TT(s5, nc, "wtd%d" % i, [128, 128], F32) for i in R4]
            Sts = [TT(s5, nc, "St%d" % i, [128, 128], BF16) for i in R4]
            wints = [TT(s5, nc, "wint%d" % i, [128, 128], F32) for i in R4]
            qss = [TT(s5, nc, "qs%d" % i, [128, 128], BF16) for i in R4]
            vws = [TT(s5, nc, "vw%d" % i, [128, 257], BF16) for i in R4]
            hms = [TT(s5, nc, "hm%d" % i, [128, 256], F32) for i in R4]
            t1s = [TT(s5, nc, "t1%d" % i, [128, 256], F32) for i in R4]
            jk5s = [TT(s5, nc, "jk5%d" % i, [128, 256], BF16) for i in R4]
            cols = [TT(s5, nc, "col%d" % i, [128, 8], F32) for i in R4]
            snp = [TT(s5, nc, "snp%d" % i, [128, 512], F32, psum=True) for i in R4]
            dcp = [TT(s5, nc, "dcp%d" % i, [128, 512], F32, psum=True) for i in R4]

            def unit(c, h, vt, own, sg_, mo_):
                cs = slice(c * 128, (c + 1) * 128)
                ur = urep[h]
                uprev = zcol[:, 0:1] if c == 0 else ur[:, c * 128 - 1:c * 128]
                uL = ur[:, (c + 1) * 128 - 1:(c + 1) * 128]
                dm, wd, St, wint, qs, vw, hm, t1, col, jk5 = (dms[h], wts_[h], Sts[h], wints[h], qss[h], vws[h], hms[h],
                                                              t1s[h], cols[h], jk5s[h])
                bk, dp_ = snp[h], dcp[h]
                sp_ = bk[:, 0:128]
                np_ = bk[:, 128:385]
                P.add("dve", _stt(dm[:], ur[:, cs], acen[:, c, h:h + 1], madd, ALU.subtract, ALU.add),
                      reads=[ur.b, acen.b, cst.b], writes=[dm.b])
                yield
                P.add("act", _act(wd[:], dm[:], AF.Exp, scale=-1.0), reads=[dm.b], writes=[wd.b])
                yield
                P.add("pe", _mm(sp_, mkT[h][:, cs], mqT[h][:, cs], True, True),
                      reads=[mkT[h].b, mqT[h].b], writes=[bk.sub("s")])
                yield
                P.add("dve", _tt(St[:], sp_, wd[:], ALU.mult), reads=[bk.sub("s"), wd.b], writes=[St.b])
                yield
                P.add("act", _act(wint[:], ur[:, cs], AF.Exp, bias=uprev, scale=-1.0), reads=[ur.b, zcol.b], writes=[wint.b])
                yield
                P.add("dve", _tt(qs[:], mqT[h][:, cs], wint[:], ALU.mult), reads=[mqT[h].b, wint.b], writes=[qs.b])
                yield
                P.add("pe", _mm(np_, St[:], vt[:, h, :], True, False),
                      reads=[St.b, vt.b, vt.sub("ones")], writes=[bk.sub("n")])
                P.add("pe", _mm(np_, qs[:], Cb[h][:], False, True), reads=[qs.b, Cb[h].b], writes=[bk.sub("n")])
                yield
                if own:
                    den = bk[:, 384:385]
                    P.add("act", _act(col[:, 0:1], den, AF.Square), reads=[bk.sub("n")], writes=[col.sub(0)])
                    yield
                    P.add("dve", _ts(col[:, 0:1], col[:, 0:1], acen[:, c, 4 + h:5 + h], None, ALU.max),
                          reads=[col.sub(0), acen.b], writes=[col.sub(0)])
                    yield
                    P.add("act", _act(col[:, 1:2], col[:, 0:1], AF.Ln), reads=[col.sub(0)], writes=[col.sub(1)])
                    yield
                    P.add("act", _act(col[:, 1:2], col[:, 1:2], AF.Exp, scale=-0.5), reads=[col.sub(1)], writes=[col.sub(1)])
                    yield
                    P.add("act", _act(hm[:], bk[:, 128:384], AF.Copy, scale=col[:, 1:2]), reads=[bk.sub("n"), col.sub(1)], writes=[hm.b])
                    yield
                    P.add("act", _act(jk5[:], hm[:], AF.Square, accum_out=col[:, 2:3]), reads=[hm.b], writes=[jk5.b, col.sub(2)])
                    yield
                    P.add("dve", _ts(col[:, 3:4], col[:, 2:3], 1.0 / 256, EPS, ALU.mult, ALU.add), reads=[col.sub(2)], writes=[col.sub(3)])
                    yield
                    P.add("act", _act(col[:, 3:4], col[:, 3:4], AF.Ln), reads=[col.sub(3)], writes=[col.sub(3)])
                    yield
                    P.add("act", _act(col[:, 3:4], col[:, 3:4], AF.Exp, scale=-0.5), reads=[col.sub(3)], writes=[col.sub(3)])
                    yield
                    P.add("dve", _stt(t1[:], hm[:], col[:, 3:4], gout[:, h, :], ALU.mult, ALU.mult),
                          reads=[hm.b, col.sub(3), gout.b], writes=[t1.b])
                    yield
                    P.add("dve", _tt(mo_[:, h * 256:(h + 1) * 256], t1[:], sg_[:, h * 256:(h + 1) * 256], ALU.mult),
                          reads=[t1.b, sg_.b], writes=[mo_.sub(h)])
                    yield
                if c < NT - 1:
                    P.add("dve", _tt(col[:, 4:5], uL, acen[:, c, h:h + 1], ALU.subtract), reads=[ur.b, acen.b], writes=[col.sub(4)])
                    yield
                    P.add("act", _act(col[:, 4:5], col[:, 4:5], AF.Exp, scale=-1.0), reads=[col.sub(4)], writes=[col.sub(4)])
                    yield
                    P.add("dve", _ts(vw[:], vt[:, h, :], col[:, 4:5], None, ALU.mult),
                          reads=[vt.b, vt.sub("ones"), col.sub(4)], writes=[vw.b])
                    yield
                    P.add("pe", _mm(dp_[:, 0:257], mk_tok[:, c, h, :], vw[:], True, True), reads=[mk_tok.b, vw.b], writes=[dp_.b])
                    yield
                    P.add("act", _act(col[:, 5:6], uL, AF.Exp, bias=uprev, scale=-1.0), reads=[ur.b, zcol.b], writes=[col.sub(5)])
                    yield
                    P.add("dve", _stt(Cf[h][:], Cf[h][:], col[:, 5:6], dp_[:, 0:257], ALU.mult, ALU.add),
                          reads=[Cf[h].b, col.sub(5), dp_.b], writes=[Cf[h].b])
                    yield
                    P.add("act", _act(Cb[h][:], Cf[h][:], AF.Copy), reads=[Cf[h].b], writes=[Cb[h].b])
                    yield

            for c in range(NT):
                vt = vtl[c % 2]
                P.add("sp", _dma(vt[:, :, 0:256], mlv_d[c * 128:(c + 1) * 128, :].rearrange("p (h e) -> p h e", h=4)),
                      writes=[vt.b], dma=True)
                own = c >= 8
                sg_ = mo_ = None
                if own:
                    sg_, mo_ = sigs[c % 2], mouts[c % 2]
                    P.add("sp", _dma(sg_[:], sigo_d[(c - 8) * 128:(c - 7) * 128, :]), writes=[sg_.b], dma=True)
                rr([unit(c, h, vt, own, sg_, mo_) for h in range(4)])
                if own:
                    P.add("sp", _dma(mix_d[(c - 8) * 128:(c - 7) * 128, 1024:2048], mo_[:]),
                          reads=[mo_.sub(h) for h in range(4)], dma=True)
        P.barrier()
        st_c.close()
        st_b.close()
        if stop_after == "S5":
            return _finish(nc, P, es, yo)

        with ExitStack() as s6:
            nmcol = TT(s6, nc, "nmcol", [128, NT], F32)
            P.add("sp", _dma(nmcol[:], nmask.rearrange("(t p) -> p t", p=128)), writes=[nmcol.b], dma=True)
            btab = TT(s6, nc, "btab", [128, 4, NT, NT], F32)
            for h in range(4):
                for kj in range(NT):
                    P.add("dve", _ts(btab[:, h, kj, :], cst[:, C_AB + h * NT:C_AB + (h + 1) * NT], nmcol[:, kj:kj + 1], None, ALU.add),
                          reads=[cst.b, nmcol.b], writes=[btab.b])
            tri2 = TT(s6, nc, "tri2", [128, 2, 128], BF16)
            for m_ in range(2):
                P.add("dve", _cp(tri2[:, m_, :], cst[:, C_TRI:C_TRI + 128]), reads=[cst.b], writes=[tri2.b])
            l4 = TT(s6, nc, "l4", [128, 4, 128], F32)
            for j, src in enumerate((lq1, lk1, lq2, lk2)):
                P.add("sp", _dma(l4[:, j, :], src.partition_broadcast(128)), writes=[l4.sub(j)], dma=True)
            lcol = TT(s6, nc, "lcol", [128, 8], F32)
            lpr = TT(s6, nc, "lpr", [128, 2, 128], F32)
            P.add("dve", _tt(lpr[:, 0, :], l4[:, 0, :], l4[:, 1, :], ALU.mult), reads=[l4.sub(0), l4.sub(1)], writes=[lpr.b])
            P.add("dve", _tt(lpr[:, 1, :], l4[:, 2, :], l4[:, 3, :], ALU.mult), reads=[l4.sub(2), l4.sub(3), lpr.b], writes=[lpr.b])
            P.add("dve", lambda e: e.reduce_sum(lcol[:, 0:2], lpr[:], AX.X), reads=[lpr.b], writes=[lcol.b])
            P.add("act", _act(lcol[:, 2:4], lcol[:, 0:2], AF.Exp), reads=[lcol.b], writes=[lcol.b])
            P.add("dve", _tt(lcol[:, 4:5], lcol[:, 3:4], lcol[:, 2:3], ALU.subtract), reads=[lcol.b], writes=[lcol.b])
            P.add("dve", _ts(lcol[:, 5:6], lcol[:, 4:5], -LAMBDA_INIT, None, ALU.add), reads=[lcol.b], writes=[lcol.b])
            neglam = lcol[:, 5:6]
            sgt = TT(s6, nc, "sgt", [128, 256], F32)
            P.add("sp", _dma(sgt[:], subln_g.partition_broadcast(128)), writes=[sgt.b], dma=True)
            P.add("dve", _ts(sgt[:], sgt[:], 1.0 - LAMBDA_INIT, None, ALU.mult), reads=[sgt.b], writes=[sgt.b])
            kTs = [[TT(s6, nc, "kT%d_%d" % (i, m_), [128, TC], BF16) for m_ in range(2)] for i in range(2)]
            qTs = [[TT(s6, nc, "qT%d_%d" % (i, m_), [128, TO], BF16) for m_ in range(2)] for i in range(2)]
            Vs = [TT(s6, nc, "V%d" % i, [128, NT, 257], BF16) for i in range(2)]
            for V in Vs:
                P.add("dve", _memset(V[:, :, 256:257], 1.0), writes=[V.sub("ones")])
            R2 = range(2)
            Pts = [TT(s6, nc, "Pt%d" % i, [128, 2, 128], BF16) for i in range(4)]
            oas = [TT(s6, nc, "oa%d" % i, [128, 256], F32) for i in R2]
            oos = [TT(s6, nc, "oo%d" % i, [128, 256], F32) for i in R2]
            aos = [TT(s6, nc, "ao%d" % i, [128, 256], BF16) for i in R2]
            jk6 = TT(s6, nc, "jk6", [128, 256], BF16)
            col6 = [TT(s6, nc, "c6%d" % i, [128, 8], F32) for i in R2]
            sps = [TT(s6, nc, "sps%d" % i, [128, 512], F32, psum=True) for i in range(4)]
            o0s = [TT(s6, nc, "o0s%d" % i, [128, 512], F32, psum=True) for i in R2]
            o1s = [TT(s6, nc, "o1s%d" % i, [128, 512], F32, psum=True) for i in R2]
            def load_head(h):
                kT, qT, V = kTs[h % 2], qTs[h % 2], Vs[h % 2]
                for m_ in range(2):
                    P.add("sp", _dma(kT[m_][:], kT_d[2 * h + m_]), writes=[kT[m_].b], dma=True)
                    P.add("sp", _dma(qT[m_][:], qT_d[2 * h + m_]), writes=[qT[m_].b], dma=True)
                P.add("sp", _dma(V[:, :, 0:256], v_d[:, h * 256:(h + 1) * 256].rearrange("(t p) c -> p t c", p=128)),
                      writes=[V.b], dma=True)

            def emit_scores(h, b, kj, r):
                kT, qT = kTs[h % 2], qTs[h % 2]
                qi = 8 + b
                sp_, Pt = sps[r], Pts[r]
                for m_ in range(2):
                    P.add("pe", _mm(sp_[:, m_ * 128:(m_ + 1) * 128], kT[m_][:, kj * 128:(kj + 1) * 128],
                                    qT[m_][:, b * 128:(b + 1) * 128], True, True),
                          reads=[kT[m_].b, qT[m_].b], writes=[sp_.b])
                P.add("act", _act(Pt[:].rearrange("p a b -> p (a b)"), sp_[:, 0:256], AF.Exp,
                                  bias=btab[:, h, kj, qi - kj:qi - kj + 1]),
                      reads=[sp_.b, btab.b], writes=[Pt.b])
                if kj == qi:
                    P.add("pool", _tt(Pt[:], Pt[:], tri2[:], ALU.mult), reads=[Pt.b, tri2.b], writes=[Pt.b])

            def emit_av(h, b, kj, r):
                V = Vs[h % 2]
                qi = 8 + b
                rb = (h * NO + b) % 2
                o0, o1 = o0s[rb], o1s[rb]
                Pt = Pts[r]
                P.add("pe", _mm(o0[:, 0:257], Pt[:, 0, :], V[:, kj, :], kj == 0, kj == qi),
                      reads=[Pt.b, V.b, V.sub("ones")], writes=[o0.b])
                P.add("pe", _mm(o1[:, 0:257], Pt[:, 1, :], V[:, kj, :], kj == 0, kj == qi),
                      reads=[Pt.b, V.b, V.sub("ones")], writes=[o1.b])
                if kj == qi:
                    epilogue(h, b)

            def epilogue(h, b):
                rb = (h * NO + b) % 2
                o0, o1 = o0s[rb], o1s[rb]
                c6, oa, oo, ao = col6[rb], oas[rb], oos[rb], aos[rb]
                P.add("dve", _ts(c6[:, 0:1], o0[:, 256:257], 1e-30, None, ALU.max), reads=[o0.b], writes=[c6.sub(0)])
                P.add("dve", _recip(c6[:, 0:1], c6[:, 0:1]), reads=[c6.sub(0)], writes=[c6.sub(0)])
                P.add("dve", _ts(c6[:, 1:2], o1[:, 256:257], 1e-30, None, ALU.max), reads=[o1.b], writes=[c6.sub(1)])
                P.add("dve", _recip(c6[:, 1:2], c6[:, 1:2]), reads=[c6.sub(1)], writes=[c6.sub(1)])
                P.add("dve", _tt(c6[:, 1:2], c6[:, 1:2], neglam, ALU.mult), reads=[c6.sub(1), lcol.b], writes=[c6.sub(1)])
                P.add("act", _act(oa[:], o0[:, 0:256], AF.Copy, scale=c6[:, 0:1]), reads=[o0.b, c6.sub(0)], writes=[oa.b])
                P.add("dve", _stt(oo[:], o1[:, 0:256], c6[:, 1:2], oa[:], ALU.mult, ALU.add),
                      reads=[o1.b, c6.sub(1), oa.b], writes=[oo.b])
                P.add("act", _act(jk6[:], oo[:], AF.Square, accum_out=c6[:, 2:3]), reads=[oo.b], writes=[jk6.b, c6.sub(2)])
                P.add("dve", _ts(c6[:, 3:4], c6[:, 2:3], 1.0 / 256, EPS, ALU.mult, ALU.add), reads=[c6.sub(2)], writes=[c6.sub(3)])
                P.add("act", _act(c6[:, 3:4], c6[:, 3:4], AF.Ln), reads=[c6.sub(3)], writes=[c6.sub(3)])
                P.add("act", _act(c6[:, 3:4], c6[:, 3:4], AF.Exp, scale=-0.5), reads=[c6.sub(3)], writes=[c6.sub(3)])
                P.add("dve", _stt(ao[:], oo[:], c6[:, 3:4], sgt[:], ALU.mult, ALU.mult),
                      reads=[oo.b, c6.sub(3), sgt.b], writes=[ao.b])
                P.add("sp", _dma(mix_d[b * 128:(b + 1) * 128, h * 256:(h + 1) * 256], ao[:]), reads=[ao.b], dma=True)

            steps = [(h, b, kj) for h in range(4) for b in range(NO) for kj in range(8 + b + 1)]
            LOOK = 2
            NR = 4
            load_head(0)
            loaded = {0}
            for i_, (h, b, kj) in enumerate(steps[:LOOK]):
                emit_scores(h, b, kj, i_ % NR)
            for i_, (h, b, kj) in enumerate(steps):
                j_ = i_ + LOOK
                if j_ < len(steps):
                    h2, b2, kj2 = steps[j_]
                    if h2 not in loaded:
                        load_head(h2)
                        loaded.add(h2)
                    emit_scores(h2, b2, kj2, j_ % NR)
                if b == 0 and kj == 0 and h + 1 < 4 and (h + 1) not in loaded:
                    load_head(h + 1)
                    loaded.add(h + 1)
                emit_av(h, b, kj, i_ % NR)
        P.barrier()
        if stop_after == "S6":
            return _finish(nc, P, es, yo)

        OWNB = [(0, 512), (512, 512), (1024, 128)]
        st_x2 = es.enter_context(ExitStack())
        xn2T = TT(st_x2, nc, "xn2T", [128, 16, TO], BF16)
        st_h1 = es.enter_context(ExitStack())
        h1 = TT(st_h1, nc, "h1", [128, NO, D], F32)
        for b in range(NO):
            P.add("sp", _dma(h1[:, b, :], xc[OWN0 + b * 128:OWN0 + (b + 1) * 128, :]), writes=[h1.sub(b)], dma=True)
        with ExitStack() as s7:
            mixT = TT(s7, nc, "mixT", [128, 16, TO], BF16)
            mts = [TT(s7, nc, "mt%d" % i, [128, D], BF16) for i in range(2)]
            wos = [TT(s7, nc, "wo%d" % i, [128, 16, 512], BF16) for i in range(2)]
            ptr7 = [TT(s7, nc, "ptr7%d" % i, [128, 8, 128], BF16, psum=True) for i in range(2)]
            pm7 = [TT(s7, nc, "pm7%d" % i, [128, 512], F32, psum=True) for i in range(3)]

            def load_wo(g):
                P.add("pool", _dma(wos[g % 2][:], w_out[:, g * 512:(g + 1) * 512].rearrange("(kt p) c -> p kt c", p=128)),
                      writes=[wos[g % 2].b], dma=True)

            load_wo(0)
            for b in range(NO):
                mt = mts[b % 2]
                P.add("sp", _dma(mt[:], mix_d[b * 128:(b + 1) * 128, :]), writes=[mt.b], dma=True)
                for half in range(2):
                    ps = ptr7[half]
                    for j in range(8):
                        kt = half * 8 + j
                        P.add("pe", _tr(ps[:, j, :], mt[:, kt * 128:(kt + 1) * 128], identb[:]), reads=[mt.b, identb.b], writes=[ps.b])
                    dst = mixT[:, half * 8:(half + 1) * 8, b * 128:(b + 1) * 128]
                    P.add("act" if half == 0 else "dve", _act(dst, ps[:], AF.Copy) if half == 0 else _cp(dst, ps[:]),
                          reads=[ps.b], writes=[mixT.sub(b)])
            k = 0
            for g in range(4):
                if g + 1 < 4:
                    load_wo(g + 1)
                wo = wos[g % 2]
                for b in range(NO):
                    ps = pm7[k % 3]
                    k += 1
                    for kt in range(16):
                        P.add("pe", _mm(ps[:], mixT[:, kt, b * 128:(b + 1) * 128], wo[:, kt, :], kt == 0, kt == 15),
                              reads=[mixT.sub(b), wo.b], writes=[ps.b])
                    dst = h1[:, b, g * 512:(g + 1) * 512]
                    P.add("dve", _tt(dst, ps[:], dst, ALU.add), reads=[ps.b, h1.sub(b)], writes=[h1.sub(b)])
            for b in range(NO):
                P.add("sp", _dma(h1_d[b * 128:(b + 1) * 128, :], h1[:, b, :]), reads=[h1.sub(b)], dma=True)
        P.barrier()
        if stop_after == "S7":
            return _finish(nc, P, es, yo)

        with ExitStack() as s8:
            g2b = TT(s8, nc, "g2b", [128, D], F32)
            P.add("sp", _dma(g2b[:], norm2_g.partition_broadcast(128)), writes=[g2b.b], dma=True)
            xnbs = [TT(s8, nc, "xnb8%d" % i, [128, D], BF16) for i in range(2)]
            junk = TT(s8, nc, "junk8", [128, D], BF16)
            ssq = [TT(s8, nc, "ssq8%d" % i, [128, 1], F32) for i in range(2)]
            rsq = [TT(s8, nc, "rsq8%d" % i, [128, 1], F32) for i in range(2)]
            ptr = [TT(s8, nc, "ptr8%d" % i, [128, 8, 128], BF16, psum=True) for i in range(2)]
            for i in range(NO):
                xnb, ss, rs = xnbs[i % 2], ssq[i % 2], rsq[i % 2]
                P.add("act", _act(junk[:], h1[:, i, :], AF.Square, accum_out=ss[:]), reads=[h1.sub(i)], writes=[junk.b, ss.b])
                P.add("dve", _ts(rs[:], ss[:], 1.0 / D, EPS, ALU.mult, ALU.add), reads=[ss.b], writes=[rs.b])
                P.add("act", _act(rs[:], rs[:], AF.Sqrt), reads=[rs.b], writes=[rs.b])
                P.add("dve", _recip(rs[:], rs[:]), reads=[rs.b], writes=[rs.b])
                P.add("dve", _stt(xnb[:], h1[:, i, :], rs[:, 0:1], g2b[:], ALU.mult, ALU.mult),
                      reads=[h1.sub(i), rs.b, g2b.b], writes=[xnb.b])
                for half in range(2):
                    ps = ptr[half]
                    for j in range(8):
                        kt = half * 8 + j
                        P.add("pe", _tr(ps[:, j, :], xnb[:, kt * 128:(kt + 1) * 128], identb[:]), reads=[xnb.b, identb.b], writes=[ps.b])
                    dst = xn2T[:, half * 8:(half + 1) * 8, i * 128:(i + 1) * 128]
                    P.add("act" if half == 0 else "dve", _act(dst, ps[:], AF.Copy) if half == 0 else _cp(dst, ps[:]),
                          reads=[ps.b], writes=[xn2T.sub(i)])
            if "xn2T" in dbg:
                o = dbgout("xn2T", [128, 16 * TO], BF16)
                P.out_dmas.append(P.add("sp", _dma(o, xn2T[:].rearrange("p k t -> p (k t)")), reads=[xn2T.sub(i) for i in range(NO)], dma=True))
        P.barrier()
        st_h1.close()
        if stop_after == "S8":
            return _finish(nc, P, es, yo)

        iota_f = cst[:, C_IOTA:C_IOTA + 128]
        with ExitStack() as s9:
            qT2 = TT(s9, nc, "qT2", [128, 16, TO], BF16)
            skT = TT(s9, nc, "skT", [128, 16, 128], BF16)
            with ExitStack() as s9a:
                wqs = [TT(s9a, nc, "wq%d" % i, [128, 16, 512], BF16) for i in range(2)]
                sk32 = TT(s9a, nc, "sk32", [128, 16, 128], F32)
                skb = TT(s9a, nc, "skb", [128, 16, 128], BF16)
                pm9 = [TT(s9a, nc, "pm9%d" % i, [128, 512], F32, psum=True) for i in range(3)]
                ptr9 = [TT(s9a, nc, "ptr9%d" % i, [128, 8, 128], BF16, psum=True) for i in range(2)]

                def load_wq(g):
                    P.add("pool", _dma(wqs[g % 2][:], wq[:, g * 512:(g + 1) * 512].rearrange("(kt p) c -> p kt c", p=128)),
                          writes=[wqs[g % 2].b], dma=True)

                load_wq(0)
                P.add("sp", _dma(sk32[:], subkeys.rearrange("a k c -> k a c")), writes=[sk32.b], dma=True)
                P.add("dve", _cp(skb[:], sk32[:]), reads=[sk32.b], writes=[skb.b])
                for half in range(2):
                    ps = ptr9[half]
                    for j in range(8):
                        P.add("pe", _tr(ps[:, j, :], skb[:, half * 8 + j, :], identb[:]), reads=[skb.b, identb.b], writes=[ps.b])
                    P.add("act", _act(skT[:, half * 8:(half + 1) * 8, :], ps[:], AF.Copy), reads=[ps.b], writes=[skT.b])
                k = 0
                for g in range(4):
                    if g + 1 < 4:
                        load_wq(g + 1)
                    wt = wqs[g % 2]
                    for sub in range(4):
                        cb = g * 4 + sub
                        for (n0, N) in OWNB:
                            ps = pm9[k % 3]
                            k += 1
                            for kt in range(16):
                                P.add("pe", _mm(ps[:, :N], wt[:, kt, sub * 128:(sub + 1) * 128], xn2T[:, kt, n0:n0 + N], kt == 0, kt == 15),
                                      reads=[wt.b] + [xn2T.sub(i) for i in range(n0 // 128, (n0 + N) // 128)], writes=[ps.b])
                            P.add("act" if k % 2 else "dve",
                                  _act(qT2[:, cb, n0:n0 + N], ps[:, :N], AF.Copy) if k % 2 else _cp(qT2[:, cb, n0:n0 + N], ps[:, :N]),
                                  reads=[ps.b], writes=[qT2.b])
            P.barrier()
            s_sbs = [TT(s9, nc, "s_sb%d" % i, [128, 16, 128], F32) for i in range(1)]
            s2t = TT(s9, nc, "s2t", [128, 128], F32)
            sv = TT(s9, nc, "sv", [128, 16, 16], F32)
            si = TT(s9, nc, "si", [128, 8, 16], U32)
            si_fs = [TT(s9, nc, "si_f%d" % i, [128, 8, 16], F32) for i in range(2)]
            cand = TT(s9, nc, "cand", [128, 16, 16], F32)
            cand2 = TT(s9, nc, "cand2", [128, 16, 16], F32)
            topv = TT(s9, nc, "topv", [128, 16], F32)
            ex16 = TT(s9, nc, "ex16", [128, 16], F32)
            c16 = TT(s9, nc, "c16", [128, 16], F32)
            e2 = TT(s9, nc, "e2", [128, 128], F32)
            thr16s = TT(s9, nc, "thr16s", [128, 8, 16], F32)
            topvs = TT(s9, nc, "topvs", [128, 8, 16], F32)
            rcs = TT(s9, nc, "rcs", [128, 8, 8], F32)
            c16s = TT(s9, nc, "c16s", [128, 8, 16], F32)
            e2s = TT(s9, nc, "e2s", [128, 8, 128], F32)
            iota_b = TT(s9, nc, "iota_b", [128, 128], BF16)
            P.add("dve", _cp(iota_b[:], iota_f), reads=[cst.b], writes=[iota_b.b])
            rc = TT(s9, nc, "rc", [128, 8], F32)
            Rtoks = [TT(s9, nc, "Rtok%d" % i, [128, 2, 16, 128], BF16) for i in range(2)]
            siTs = [TT(s9, nc, "siT%d" % i, [128, 128], BF16) for i in range(2)]
            Rts = [TT(s9, nc, "Rt%d" % i, [128, 32, 128], BF16) for i in range(2)]
            OHs = [TT(s9, nc, "OH%d" % i, [128, 32, 128], BF16) for i in range(2)]
            GTs = TT(s9, nc, "GTs", [128, 128, 128], BF16)
            psc = [TT(s9, nc, "psc%d" % i, [128, 4, 128], F32, psum=True) for i in range(1)]
            pst = TT(s9, nc, "pst", [128, 128], F32, psum=True)
            pgt = [TT(s9, nc, "pgt%d" % i, [128, 128, 8], F32, psum=True) for i in range(1)]
            NCH = NEXP // 128
            OWNB3 = [(TO - NOWN, 344), (TO - NOWN + 344, 344), (TO - NOWN + 688, 344)]
            ucs9 = [TT(s9, nc, "uc9%d" % i, [128, D], BF16) for i in range(3)]
            ucT9 = [TT(s9, nc, "ucT9%d" % i, [128, 16, 128], BF16) for i in range(2)]
            glb = [TT(s9, nc, "glb%d" % i, [128, NOWN], BF16) for i in range(2)]
            pas9 = [TT(s9, nc, "pas9%d" % i, [128, 512], F32, psum=True) for i in range(3)]
            ptu9 = TT(s9, nc, "ptu9", [128, 8, 128], BF16, psum=True)
            ust = {"next": 0}

            def u_load(i):
                P.add("pool", _dma(ucs9[i % 3][:], pu[i * 128:(i + 1) * 128, :]), writes=[ucs9[i % 3].b], dma=True)

            def u_tr(i, half):
                uc, ucT = ucs9[i % 3], ucT9[i % 2]
                for j in range(8):
                    kt = half * 8 + j
                    P.add("pe", _tr(ptu9[:, j, :], uc[:, kt * 128:(kt + 1) * 128], identb[:]), reads=[uc.b, identb.b], writes=[ptu9.b])
                P.add("act", _act(ucT[:, half * 8:(half + 1) * 8, :], ptu9[:], AF.Copy), reads=[ptu9.b], writes=[ucT.sub(half)])

            def u_blk(i, bi):
                n0, N = OWNB3[bi]
                ucT, gl = ucT9[i % 2], glb[i % 2]
                pa = pas9[bi]
                for kt in range(16):
                    P.add("pe", _mm(pa[:, :N], ucT[:, kt, :], xn2T[:, kt, n0:n0 + N], kt == 0, kt == 15),
                          reads=[ucT.sub(kt // 8)] + [xn2T.sub(t) for t in range(n0 // 128, (n0 + N - 1) // 128 + 1)], writes=[pa.b])
                P.add("act", _act(gl[:, bi * 344:(bi + 1) * 344], pa[:, :N], AF.Gelu), reads=[pa.b], writes=[gl.sub(bi)])

            def u_unit():
                i = ust["next"]
                if i >= NCH:
                    return
                ust["next"] = i + 1
                if i + 2 < NCH:
                    u_load(i + 2)
                if i + 1 < NCH:
                    u_tr(i + 1, 0)
                u_blk(i, 0)
                u_blk(i, 1)
                if i + 1 < NCH:
                    u_tr(i + 1, 1)
                u_blk(i, 2)
                P.add("sp", _dma(act_d[i], glb[i % 2][:]), reads=[glb[i % 2].sub(k) for k in range(3)], dma=True)

            def filler(n):
                for _ in range(n):
                    u_unit()

            u_load(0)
            u_load(1)
            u_tr(0, 0)
            u_tr(0, 1)
            R_buf = [Buf("R_d%d" % b) for b in range(NO)]
            def phaseA1(b):
                ts_ = slice(b * 128, (b + 1) * 128)
                s_sb = s_sbs[0]
                for q4 in range(4):
                    for j in range(4):
                        hp = q4 * 4 + j
                        P.add("pe", _mm(psc[0][:, j, :], qT2[:, hp, ts_], skT[:, hp, :], True, True),
                              reads=[qT2.b, skT.b], writes=[psc[0].b])
                    P.add("act", _act(s_sb[:, q4 * 4:(q4 + 1) * 4, :], psc[0][:], AF.Copy), reads=[psc[0].b], writes=[s_sb.sub(q4)])

            def phaseA(b):
                ts_ = slice(b * 128, (b + 1) * 128)
                siT = siTs[b % 2]
                s_sb = s_sbs[0]
                for hp in range(16):
                    h, p = hp // 2, hp % 2
                    sb_ = s_sb.sub(hp // 4)
                    P.add("dve", lambda e, hp=hp: e.max(sv[:, hp, 0:8], s_sb[:, hp, :]), reads=[sb_], writes=[sv.sub(hp)])
                    if p == 0:
                        P.add("dve", lambda e, hp=hp, h=h: e.max_index(si[:, h, 0:8], sv[:, hp, 0:8], s_sb[:, hp, :]),
                              reads=[sb_, sv.sub(hp)], writes=[si.sub(h)])
                    P.add("dve", lambda e, hp=hp: e.match_replace(s2t[:], sv[:, hp, 0:8], s_sb[:, hp, :], NEG),
                          reads=[sb_, sv.sub(hp)], writes=[s2t.b])
                    P.add("dve", lambda e, hp=hp: e.max(sv[:, hp, 8:16], s2t[:]), reads=[s2t.b], writes=[sv.sub(hp)])
                    if p == 0:
                        P.add("dve", lambda e, hp=hp, h=h: e.max_index(si[:, h, 8:16], sv[:, hp, 8:16], s2t[:]),
                              reads=[s2t.b, sv.sub(hp)], writes=[si.sub(h)])
                for h in range(8):
                    hp1, hp2 = 2 * h, 2 * h + 1
                    tv = topvs[:, h, :]
                    P.add("dve", _tt(cand[:], sv[:, hp1, :].unsqueeze(2).to_broadcast([128, 16, 16]),
                                     sv[:, hp2, :].unsqueeze(1).to_broadcast([128, 16, 16]), ALU.add),
                          reads=[sv.sub(hp1), sv.sub(hp2)], writes=[cand.b])
                    P.add("dve", lambda e, h=h: e.max(topvs[:, h, 0:8], cand[:]), reads=[cand.b], writes=[topvs.sub(h)])
                    P.add("dve", lambda e, h=h: e.match_replace(cand2[:], topvs[:, h, 0:8], cand[:], NEG),
                          reads=[cand.b, topvs.sub(h)], writes=[cand2.b])
                    P.add("dve", lambda e, h=h: e.max(topvs[:, h, 8:16], cand2[:]), reads=[cand2.b], writes=[topvs.sub(h)])
                    P.add("dve", _ts(rcs[:, h, 0:1], tv[:, 0:1], -1.0, None, ALU.mult), reads=[topvs.sub(h)], writes=[rcs.sub((h, 0))])
                    P.add("dve", _ts(rcs[:, h, 3:4], sv[:, hp1, 0:1], -1.0, None, ALU.mult), reads=[sv.sub(hp1)], writes=[rcs.sub((h, 3))])
                    P.add("dve", _ts(rcs[:, h, 4:5], sv[:, hp2, 0:1], -1.0, None, ALU.mult), reads=[sv.sub(hp2)], writes=[rcs.sub((h, 4))])
                    P.add("dve", _ts(rcs[:, h, 5:6], tv[:, 15:16], -2e-6, None, ALU.add), reads=[topvs.sub(h)], writes=[rcs.sub((h, 5))])
                    P.add("dve", _ts(thr16s[:, h, :], sv[:, hp1, :], -1.0, rcs[:, h, 5:6], ALU.mult, ALU.add),
                          reads=[sv.sub(hp1), rcs.sub((h, 5))], writes=[thr16s.sub(h)])
                yield
                for h in range(8):
                    hp1, hp2 = 2 * h, 2 * h + 1
                    P.add("act", _act(ex16[:], topvs[:, h, :], AF.Exp, bias=rcs[:, h, 0:1], accum_out=rcs[:, h, 1:2]),
                          reads=[topvs.sub(h), rcs.sub((h, 0))], writes=[ex16.b, rcs.sub((h, 1))])
                    P.add("act", _act(c16s[:, h, :], sv[:, hp1, :], AF.Exp, bias=rcs[:, h, 3:4]),
                          reads=[sv.sub(hp1), rcs.sub((h, 3))], writes=[c16s.sub(h)])
                    P.add("act", _act(e2s[:, h, :], s_sb[:, hp2, :], AF.Exp, bias=rcs[:, h, 4:5]),
                          reads=[s_sb.sub(hp2 // 4), rcs.sub((h, 4))], writes=[e2s.sub(h)])
                for h in range(8):
                    hp1, hp2 = 2 * h, 2 * h + 1
                    Rtok = Rtoks[(h // 2) % 2]
                    hl = h % 2
                    P.add("dve", _recip(rcs[:, h, 2:3], rcs[:, h, 1:2]), reads=[rcs.sub((h, 1))], writes=[rcs.sub((h, 2))])
                    P.add("dve", _ts(c16s[:, h, :], c16s[:, h, :], rcs[:, h, 2:3], None, ALU.mult),
                          reads=[c16s.sub(h), rcs.sub((h, 2))], writes=[c16s.sub(h)])
                    for a_ in range(16):
                        P.add("dve", _stt(Rtok[:, hl, a_, :], s_sb[:, hp2, :], thr16s[:, h, a_:a_ + 1], e2s[:, h, :], ALU.is_ge, ALU.mult),
                              reads=[s_sb.sub(hp2 // 4), thr16s.sub(h), e2s.sub(h)], writes=[Rtok.sub((hl, a_))])
                    P.add("pool", _tt(Rtok[:, hl, :, :], Rtok[:, hl, :, :], c16s[:, h, :].unsqueeze(2).to_broadcast([128, 16, 128]), ALU.mult),
                          reads=[Rtok.sub((hl, a_)) for a_ in range(16)] + [c16s.sub(h)],
                          writes=[Rtok.sub(hl)] + [Rtok.sub((hl, a_)) for a_ in range(16)])
                    if hl == 1:
                        hp_ = h // 2
                        P.add("sp", _dma(R_d[ts_, hp_ * 32:(hp_ + 1) * 32, :], Rtok[:].rearrange("p a b c -> p (a b) c")),
                              reads=[Rtok.sub(0), Rtok.sub(1)] + [Rtok.sub((hl_, a_)) for hl_ in range(2) for a_ in range(16)],
                              writes=[R_buf[b]], dma=True)
                        yield
                P.add("dve", _cp(si_fs[b % 2][:], si[:]), reads=[si.sub(h) for h in range(8)], writes=[si_fs[b % 2].b])

            def phaseB(b):
                ts_ = slice(b * 128, (b + 1) * 128)
                siT = siTs[b % 2]
                si_f = si_fs[b % 2]
                P.add("pe", _tr(pst[:], si_f[:].rearrange("p a b -> p (a b)"), ident_f), reads=[si_f.b, cst.b], writes=[pst.b])
                P.add("act", _act(siT[:], pst[:], AF.Copy), reads=[pst.b], writes=[siT.b])
                def rt_load(hf_):
                    t0_ = b * 128 + hf_ * 32
                    P.add("sp", _dma(Rts[hf_ % 2][:], R_d[t0_:t0_ + 32].rearrange("t p j -> p t j")), reads=[R_buf[b]],
                          writes=[Rts[hf_ % 2].b], dma=True)

                rt_load(0)
                for hf in range(4):
                    Rt, OH = Rts[hf % 2], OHs[hf % 2]
                    if hf + 1 < 4:
                        rt_load(hf + 1)
                    P.add("dve", _tt(OH[:], iota_b[:].unsqueeze(1).to_broadcast([128, 32, 128]),
                                      siT[:, hf * 32:(hf + 1) * 32].unsqueeze(2).to_broadcast([128, 32, 128]), ALU.is_equal),
                          reads=[iota_b.b, siT.b], writes=[OH.b])
                    for t8 in range(4):
                        pg = pgt[0]
                        for tl in range(8):
                            t = t8 * 8 + tl
                            for fh in range(2):
                                P.add("pe", _mm(pg[:, fh * 64:(fh + 1) * 64, tl], Rt[:, t, :], OH[:, t, fh * 64:(fh + 1) * 64], True, True),
                                      reads=[Rt.b, OH.b], writes=[pg.b])
                        tt0 = hf * 32 + t8 * 8
                        P.add("act", _act(GTs[:, :, tt0:tt0 + 8], pg[:], AF.Copy), reads=[pg.b], writes=[GTs.sub(hf)])
                        if (hf * 4 + t8) % 3 != 2:
                            filler(1)
                    if hf < 3:
                        yield
                P.add("sp", _dma(GT_d[:, :, ts_].rearrange("i j t -> j i t"), GTs[:]), reads=[GTs.sub(i) for i in range(4)], dma=True)

            phaseA1(0)
            for b in range(NO):
                gA = phaseA(b)
                gB = phaseB(b - 1) if b >= 1 else iter(())
                next(gB, None)
                next(gB, None)
                next(gA)
                filler(3)
                next(gB, None)
                for _ in gB:
                    pass
                for k_, _ in enumerate(gA):
                    if k_ in (0, 1, 2, 3):
                        filler(1)
                if b + 1 < NO:
                    phaseA1(b + 1)
                if b == 0:
                    filler(8)
            for _ in phaseB(NO - 1):
                pass
            filler(NCH)
        P.barrier()
        if stop_after == "S9":
            return _finish(nc, P, es, yo)

        with ExitStack() as s10:
            yacc = TT(s10, nc, "yacc", [128, NO, D], F32)
            for b in range(NO):
                P.add("sp", _dma(yacc[:, b, :], h1_d[b * 128:(b + 1) * 128, :]), writes=[yacc.sub(b)], dma=True)
            vcs = [TT(s10, nc, "vc%d" % i, [128, GC, D], BF16) for i in range(2)]
            gts = [TT(s10, nc, "gts%d" % i, [128, TO], BF16) for i in range(GC)]
            acs = [TT(s10, nc, "acs%d" % i, [128, NOWN], BF16) for i in range(GC)]
            Wts = [TT(s10, nc, "Wt%d" % i, [128, GC, TO], BF16) for i in range(2)]
            for Wt in Wts:
                P.add("dve", _memset(Wt[:, :, 0:TO - NOWN], 0.0), writes=[Wt.sub("z")])
            pys = [TT(s10, nc, "py%d" % i, [128, 512], F32, psum=True) for i in range(4)]
            cnt10 = {"y": 0}

            def load_a(i):
                P.add("sp", _dma(acs[i % GC][:], act_d[i]), writes=[acs[i % GC].b], dma=True)
                P.add("sp", _dma(gts[i % GC][:], GT_d[i]), writes=[gts[i % GC].b], dma=True)

            def load_v(g, c):
                r0 = (g * GC + c) * 128
                P.add("pool", _dma(vcs[g % 2][:, c, :], pv[r0:r0 + 128, :]), writes=[vcs[g % 2].sub(c)], dma=True)

            def gate(i):
                g, ci = i // GC, i % GC
                Wt = Wts[g % 2]
                P.add("dve" if i % 2 else "pool", _tt(Wt[:, ci, TO - NOWN:TO], acs[i % GC][:], gts[i % GC][:, TO - NOWN:TO], ALU.mult),
                      reads=[acs[i % GC].b, gts[i % GC].b], writes=[Wt.sub(ci)])

            def down_proj(g):
                Wt, vc = Wts[g % 2], vcs[g % 2]
                for b in range(NO):
                    for gq_ in range(4):
                        py = pys[cnt10["y"] % 4]
                        cnt10["y"] += 1
                        for c in range(GC):
                            P.add("pe", _mm(py[:], Wt[:, c, b * 128:(b + 1) * 128], vc[:, c, gq_ * 512:(gq_ + 1) * 512], c == 0, c == GC - 1),
                                  reads=[Wt.sub(c), Wt.sub("z"), vc.sub(c)], writes=[py.b])
                        dst = yacc[:, b, gq_ * 512:(gq_ + 1) * 512]
                        P.add("dve", _tt(dst, py[:], dst, ALU.add), reads=[py.b, yacc.sub(b)], writes=[yacc.sub(b)])

            NG = NCH // GC
            for c in range(GC):
                load_v(0, c)
                load_a(c)
            for c in range(GC):
                gate(c)
            for g in range(NG):
                if g + 1 < NG:
                    for c in range(GC):
                        load_v(g + 1, c)
                        load_a((g + 1) * GC + c)
                    for c in range(GC):
                        gate((g + 1) * GC + c)
                down_proj(g)
            for b in range(NO):
                P.out_dmas.append(P.add("sp", _dma(yo[b * 128:(b + 1) * 128, :], yacc[:, b, :]), reads=[yacc.sub(b)], dma=True))
        return _finish(nc, P, es, yo)


def _finish(nc, P, es, yo):
    P.emit(es)
    return nc


def make_in_maps(inputs, cores=range(8)):
    x = np.asarray(inputs["x"], np.float32)
    meta = np.asarray(inputs["meta_tokens"], np.float32)
    consts = make_consts()
    shared = {
        "consts": consts,
        "norm1_g": inputs["norm1_g"][0], "w_in": inputs["w_in"][0],
        "ml_conv_w": inputs["ml_conv_w"][0], "ml_conv_b": inputs["ml_conv_b"][0],
        "ml_igate_b": inputs["ml_igate_b"][0], "ml_fgate_b": inputs["ml_fgate_b"][0],
        "da_lq1": inputs["da_lq1"][0], "da_lk1": inputs["da_lk1"][0],
        "da_lq2": inputs["da_lq2"][0], "da_lk2": inputs["da_lk2"][0],
        "da_qnorm_g": inputs["da_qnorm_g"][0], "da_knorm_g": inputs["da_knorm_g"][0],
        "da_subln_g": inputs["da_subln_g"][0], "ml_outnorm_g": inputs["ml_outnorm_g"][0],
        "w_out": inputs["w_out"][0], "norm2_g": inputs["norm2_g"][0],
        "peer_wq": inputs["peer_wq"][0],
        "peer_subkeys": inputs["peer_subkeys"][0].reshape(16, 128, 128),
        "peer_u": inputs["peer_u"][0], "peer_v": inputs["peer_v"][0],
    }
    shared = {k: np.ascontiguousarray(np.asarray(v, np.float32)) for k, v in shared.items()}
    maps = []
    for c in cores:
        b, hf = c // 2, c % 2
        ntok = 1032 * (hf + 1)
        h = np.concatenate([meta, x[b]], axis=0)[:ntok]
        xc = np.zeros((TC, D), np.float32)
        xc[TC - ntok:] = h
        nm = np.full((TC,), NEG, np.float32)
        nm[TC - ntok:] = 0.0
        va = np.zeros((TC,), np.float32)
        va[TC - ntok:] = 1.0
        m = dict(shared)
        m.update({"xc": xc, "nmask": nm, "valid": va})
        maps.append(m)
    return maps


_NC_CACHE = {}


def kernel(**inputs):
    if "nc" not in _NC_CACHE:
        _NC_CACHE["nc"] = build()
    nc = _NC_CACHE["nc"]
    maps = make_in_maps(inputs)
    res = run_bass_kernel_spmd(nc, maps, core_ids=list(range(8)))
    out = np.zeros((4, 2048, 2048), np.float32)
    for c in range(8):
        b, hf = c // 2, c % 2
        y = np.asarray(res.results[c]["yo"], np.float32)[TO - NOWN:]
        if hf == 0:
            out[b, 0:1016] = y[16:]
        else:
            out[b, 1016:2048] = y
    return out
```
